# Optimizing a Trainium2 kernel written in Bass

```python
import math
import jax, jax.numpy as jnp
from jax import lax
import numpy as np

D_MODEL = 1024
BATCH = 16
SEQ = 256
DEPTH = 4
DEC_BATCH = 8
DEC_SEQ = 4096
PAST_LEN = 256

GRID_W = 64
HEAD_DIM = 64
N_ATT_HEADS = 4
ATT_WIDTH = N_ATT_HEADS * 2 * HEAD_DIM
N_FOURIER_GROUPS = 4
FOURIER_GROUP_DIM = 64
FOURIER_WIDTH = N_FOURIER_GROUPS * FOURIER_GROUP_DIM
N_GMLP_HEADS = 4
GMLP_HEAD_DIM = 64
GMLP_WIDTH = N_GMLP_HEADS * GMLP_HEAD_DIM
CHUNK = 128
MIX_WIDTH = ATT_WIDTH + FOURIER_WIDTH + GMLP_WIDTH
IN_WIDTH = 4 * ATT_WIDTH + 2 * FOURIER_WIDTH + 3 * GMLP_WIDTH
ROPE_THETA = 10000.0
DEEPNORM_ALPHA = (2.0 * DEPTH) ** 0.25
DEEPNORM_BETA = (8.0 * DEPTH) ** -0.25
LN_EPS = 1e-5
RMS_EPS = 1e-5
BLOCK_Q = 128

kernel_name = 'hybrid_diffattn_fnet_gmlp_flow_step'


def _layernorm(x):
    xf = x.astype(jnp.float32)
    mu = jnp.mean(xf, axis=-1, keepdims=True)
    var = jnp.mean(jnp.square(xf - mu), axis=-1, keepdims=True)
    return (xf - mu) * lax.rsqrt(var + LN_EPS)


def _adaln_input(x, mod):
    shift, scale, gate = jnp.split(mod, 3, axis=-1)
    h = _layernorm(x) * (1.0 + scale[:, None, :]) + shift[:, None, :]
    return h.astype(x.dtype), gate[:, None, :]


def _split_proj(z):
    sizes = (ATT_WIDTH,) * 4 + (FOURIER_WIDTH,) * 2 + (GMLP_WIDTH,) * 3
    idx = np.cumsum(sizes)[:-1].tolist()
    return jnp.split(z, idx, axis=-1)


def _heads_qk(t):
    b, l, _ = t.shape
    return t.reshape(b, l, N_ATT_HEADS, 2, HEAD_DIM).transpose(0, 2, 1, 3, 4)


def _heads_v(t):
    b, l, _ = t.shape
    return t.reshape(b, l, N_ATT_HEADS, 2 * HEAD_DIM).transpose(0, 2, 1, 3)


def _lambda(lq1, lk1, lq2, lk2, lam_init):
    f = jnp.float32
    return (jnp.exp(jnp.sum(lq1.astype(f) * lk1.astype(f)))
            - jnp.exp(jnp.sum(lq2.astype(f) * lk2.astype(f))) + lam_init)


def _diff_attention(q, k, v, lam, subln_w, lam_init):
    s = jnp.einsum('bhqmd,bhkmd->bhmqk', q, k,
                   preferred_element_type=jnp.float32) * (HEAD_DIM ** -0.5)
    p = jax.nn.softmax(s, axis=-1)
    a = p[:, :, 0] - lam * p[:, :, 1]
    o = jnp.einsum('bhqk,bhke->bhqe', a, v.astype(jnp.float32))
    o = (o * lax.rsqrt(jnp.mean(jnp.square(o), axis=-1, keepdims=True) + RMS_EPS)
         * subln_w.astype(jnp.float32) * (1.0 - lam_init))
    return o.astype(v.dtype)


def _axial_rope_tables(rows):
    n_freq = HEAD_DIM // 4
    row = jnp.repeat(jnp.arange(rows, dtype=jnp.float32), GRID_W)
    col = jnp.tile(jnp.arange(GRID_W, dtype=jnp.float32), rows)
    inv = ROPE_THETA ** (-jnp.arange(n_freq, dtype=jnp.float32) / n_freq)
    ang_r = (row[:, None] * inv)[:, None, :]
    ang_c = (col[:, None] * inv)[:, None, :]
    return (jnp.cos(ang_r), jnp.sin(ang_r), jnp.cos(ang_c), jnp.sin(ang_c))


def _rot_half(x, cos, sin):
    x1, x2 = jnp.split(x, 2, axis=-1)
    return jnp.concatenate([x1 * cos - x2 * sin, x2 * cos + x1 * sin], axis=-1)


def _rope_2d(x, tables):
    cr, sr, cc, sc = tables
    xr, xc = jnp.split(x, 2, axis=-1)
    return jnp.concatenate([_rot_half(xr, cr, sr), _rot_half(xc, cc, sc)], axis=-1).astype(x.dtype)


def _fourier_mix(f, w_f):
    b, l, _ = f.shape
    fg = f.reshape(b, l, N_FOURIER_GROUPS, FOURIER_GROUP_DIM).astype(jnp.float32)
    re = jnp.real(jnp.fft.fft2(fg, axes=(1, 3), norm='ortho'))
    out = jnp.einsum('blgc,gce->blge', re, w_f.astype(jnp.float32))
    return out.reshape(b, l, FOURIER_WIDTH).astype(f.dtype)


def _gmlp_spatial(u, vm, w_s, b_s):
    b, l, _ = vm.shape
    n = l // CHUNK
    vn = _layernorm(vm.reshape(b, n, CHUNK, N_GMLP_HEADS, GMLP_HEAD_DIM))
    s = (jnp.einsum('hpq,bnqhc->bnphc', w_s.astype(jnp.float32), vn)
         + b_s.astype(jnp.float32).T[:, :, None])
    return (u.astype(jnp.float32) * s.reshape(b, l, GMLP_WIDTH)).astype(u.dtype)


def _local_branches(f, g_f, u, vm, g_m, w_f, w_s, b_s):
    return jnp.concatenate([_fourier_mix(f, w_f) * jax.nn.silu(g_f),
                            _gmlp_spatial(u, vm, w_s, b_s) * jax.nn.silu(g_m)], axis=-1)


def _finish(x, att, g_att, local_out, gate, w_out, ln_g, ln_b):
    mixed = jnp.concatenate([att * jax.nn.silu(g_att), local_out], axis=-1)
    y = mixed @ w_out
    r = DEEPNORM_ALPHA * x + gate * y
    return (_layernorm(r) * ln_g + ln_b).astype(x.dtype)


def _context_layer(x, c_ctx, lam_init, w_ada, b_ada, w_in, w_out, lq1, lk1, lq2, lk2,
                   subln_w, fourier_w, gmlp_ws, gmlp_bs, ln_g, ln_b):
    b, l, _ = x.shape
    mod = jax.nn.silu(c_ctx)[None, :] @ w_ada + b_ada
    h, gate = _adaln_input(x, mod)
    q, k, v, g_att, f, g_f, u, vm, g_m = _split_proj(h @ w_in)
    q, k, v = _heads_qk(q), _heads_qk(k), _heads_v(v)
    lam = _lambda(lq1, lk1, lq2, lk2, lam_init)
    att = _diff_attention(q, k, v, lam, subln_w, lam_init)
    att = att.transpose(0, 2, 1, 3).reshape(b, l, ATT_WIDTH)
    local_out = _local_branches(f, g_f, u, vm, g_m, fourier_w, gmlp_ws, gmlp_bs)
    x_new = _finish(x, att, g_att, local_out, gate, w_out, ln_g, ln_b)
    return x_new, k.reshape(b, N_ATT_HEADS, l, 2 * HEAD_DIM), v


def _latent_layer(x, c, ctx_k, ctx_v, tables, lam_init, w_ada, b_ada, w_in, w_out,
                  lq1, lk1, lq2, lk2, subln_w, fourier_w, gmlp_ws, gmlp_bs, ln_g, ln_b):
    b, l, _ = x.shape
    lc = ctx_k.shape[2]
    mod = jax.nn.silu(c) @ w_ada + b_ada
    h, gate = _adaln_input(x, mod)
    q, k, v, g_att, f, g_f, u, vm, g_m = _split_proj(h @ w_in)
    q = _rope_2d(_heads_qk(q), tables)
    k = _rope_2d(_heads_qk(k), tables)
    v = _heads_v(v)
    k_all = jnp.concatenate([ctx_k.reshape(b, N_ATT_HEADS, lc, 2, HEAD_DIM).astype(k.dtype), k], axis=2)
    v_all = jnp.concatenate([ctx_v.astype(v.dtype), v], axis=2)
    lam = _lambda(lq1, lk1, lq2, lk2, lam_init)
    n_blk = l // BLOCK_Q
    q_blocks = q.reshape(b, N_ATT_HEADS, n_blk, BLOCK_Q, 2, HEAD_DIM).transpose(2, 0, 1, 3, 4, 5)
    att = lax.map(lambda qb: _diff_attention(qb, k_all, v_all, lam, subln_w, lam_init), q_blocks)
    att = att.transpose(1, 0, 3, 2, 4).reshape(b, l, ATT_WIDTH)
    local_out = _local_branches(f, g_f, u, vm, g_m, fourier_w, gmlp_ws, gmlp_bs)
    return _finish(x, att, g_att, local_out, gate, w_out, ln_g, ln_b)


def setup_inputs(seed: int = 0) -> dict:
    key = jax.random.key(seed)
    ks = jax.random.split(key, 20)
    f32 = jnp.float32

    def nrm(k, shape, s):
        return jax.random.normal(k, shape, f32) * s

    return {
        'x_prompt': nrm(ks[0], (BATCH, SEQ, D_MODEL), 1.0),
        'x_sample': nrm(ks[1], (DEC_BATCH, DEC_SEQ, D_MODEL), 1.0),
        'c': nrm(ks[2], (DEC_BATCH, D_MODEL), 1.0),
        'cache_k': nrm(ks[3], (DEC_BATCH, DEPTH, N_ATT_HEADS, PAST_LEN, 2 * HEAD_DIM), 1.0),
        'cache_v': nrm(ks[4], (DEC_BATCH, DEPTH, N_ATT_HEADS, PAST_LEN, 2 * HEAD_DIM), 1.0),
        'c_ctx': nrm(ks[5], (D_MODEL,), 1.0),
        'w_ada': nrm(ks[6], (DEPTH, D_MODEL, 3 * D_MODEL), D_MODEL ** -0.5),
        'b_ada': nrm(ks[7], (DEPTH, 3 * D_MODEL), 0.02),
        'w_in': nrm(ks[8], (DEPTH, D_MODEL, IN_WIDTH), D_MODEL ** -0.5),
        'w_out': nrm(ks[9], (DEPTH, MIX_WIDTH, D_MODEL), DEEPNORM_BETA * MIX_WIDTH ** -0.5),
        'lam_q1': nrm(ks[10], (DEPTH, HEAD_DIM), 0.1),
        'lam_k1': nrm(ks[11], (DEPTH, HEAD_DIM), 0.1),
        'lam_q2': nrm(ks[12], (DEPTH, HEAD_DIM), 0.1),
        'lam_k2': nrm(ks[13], (DEPTH, HEAD_DIM), 0.1),
        'subln_w': 1.0 + nrm(ks[14], (DEPTH, 2 * HEAD_DIM), 0.02),
        'fourier_w': nrm(ks[15], (DEPTH, N_FOURIER_GROUPS, FOURIER_GROUP_DIM, FOURIER_GROUP_DIM), FOURIER_GROUP_DIM ** -0.5),
        'gmlp_ws': nrm(ks[16], (DEPTH, N_GMLP_HEADS, CHUNK, CHUNK), CHUNK ** -0.5),
        'gmlp_bs': nrm(ks[17], (DEPTH, N_GMLP_HEADS, CHUNK), 0.02),
        'ln_g': 1.0 + nrm(ks[18], (DEPTH, D_MODEL), 0.02),
        'ln_b': nrm(ks[19], (DEPTH, D_MODEL), 0.02),
    }


def reference(x_prompt, x_sample, c, cache_k, cache_v, c_ctx, w_ada, b_ada, w_in, w_out,
              lam_q1, lam_k1, lam_q2, lam_k2, subln_w, fourier_w, gmlp_ws, gmlp_bs, ln_g, ln_b):
    rows = x_sample.shape[1] // GRID_W
    tables = _axial_rope_tables(rows)

    xp = x_prompt
    ks_new, vs_new = [], []
    for l in range(DEPTH):
        lam_init = 0.8 - 0.6 * math.exp(-0.3 * l)
        xp, k_l, v_l = _context_layer(
            xp, c_ctx, lam_init, w_ada[l], b_ada[l], w_in[l], w_out[l],
            lam_q1[l], lam_k1[l], lam_q2[l], lam_k2[l], subln_w[l],
            fourier_w[l], gmlp_ws[l], gmlp_bs[l], ln_g[l], ln_b[l])
        ks_new.append(k_l)
        vs_new.append(v_l)
    new_k = jnp.stack(ks_new, axis=1)
    new_v = jnp.stack(vs_new, axis=1)

    xs = x_sample
    for l in range(DEPTH):
        lam_init = 0.8 - 0.6 * math.exp(-0.3 * l)
        xs = _latent_layer(
            xs, c, cache_k[:, l], cache_v[:, l], tables, lam_init,
            w_ada[l], b_ada[l], w_in[l], w_out[l],
            lam_q1[l], lam_k1[l], lam_q2[l], lam_k2[l], subln_w[l],
            fourier_w[l], gmlp_ws[l], gmlp_bs[l], ln_g[l], ln_b[l])

    return (xp, xs, new_k, new_v)
```

```python
import math
from contextlib import ExitStack

import numpy as np
import ml_dtypes
import concourse.bass as bass
import concourse.mybir as mybir
from concourse.bass_utils import run_bass_kernel_spmd

F32 = mybir.dt.float32
BF16 = mybir.dt.bfloat16
ALU = mybir.AluOpType
AF = mybir.ActivationFunctionType

D = 1024
H = 4
NPAST = 256
LN_EPS = 1e-5
RMS_EPS = 1e-5
DEPTH = 4
ALPHA = (2.0 * DEPTH) ** 0.25
N_CORES = 8


class Sem:
    def __init__(self, h, name):
        self.h = h
        self.name = name
        self.count = 0


class Buf:
    def __init__(self, name):
        self.name = name
        self.w = {}
        self.r = {}
        self.sem = None


class Eng:
    LIMIT = 24000

    def __init__(self, fw, name, h, skip_self=False):
        self.fw = fw
        self.name = name
        self.h = h
        self.skip_self = skip_self
        self.own = set()
        self.known = {}
        self.pending = False
        self.sem = None
        self._new_sem()

    def _new_sem(self):
        self.sem = self.fw.new_sem("e_" + self.name)
        self.own.add(self.sem)

    def wait(self, sem, val):
        if val <= 0:
            return
        if self.skip_self and sem in self.own:
            return
        if self.known.get(sem, 0) >= val:
            return
        assert sem.count >= val, f"wait on unsignalled token {sem.name} {val}>{sem.count} from {self.name}"
        self.h.wait_ge(sem.h, val)
        self.known[sem] = val

    def signal(self, inst, signal=True):
        if signal:
            inst.then_inc(self.sem.h, 1)
            self.sem.count += 1
            tok = (self.sem, self.sem.count)
            self.pending = False
            if self.sem.count >= self.LIMIT:
                self._new_sem()
            return tok
        self.pending = True
        return (self.sem, self.sem.count + 1)


class FW:
    def __init__(self, nc, stack):
        self.nc = nc
        self.stack = stack
        self.nsem = 0
        self.pool = []
        self.uid = 0
        self.pe = Eng(self, "pe", nc.tensor, skip_self=True)
        self.act = Eng(self, "act", nc.scalar)
        self.dve = Eng(self, "dve", nc.vector)
        self.pool_e = Eng(self, "pool", nc.gpsimd)
        self.sp = Eng(self, "sp", nc.sync)
        self.engs = [self.pe, self.act, self.dve, self.pool_e, self.sp]
        self.dma_sems = []

    def new_sem(self, name):
        self.nsem += 1
        h = self.stack.enter_context(self.nc.semaphore(f"{name}_{self.nsem}"))
        return Sem(h, name)

    def dma_sem(self):
        if self.pool:
            return self.pool.pop()
        s = self.new_sem("d")
        self.dma_sems.append(s)
        return s

    def release(self, bufs):
        for b in bufs:
            if b.sem is not None:
                self.pool.append(b.sem)
                b.sem = None

    def name(self, p):
        self.uid += 1
        return f"{p}_{self.uid}"

    def op(self, eng, fn, reads=(), writes=(), signal=True):
        for b in reads:
            for s, v in b.w.items():
                eng.wait(s, v)
        for b in writes:
            for s, v in b.w.items():
                eng.wait(s, v)
            for s, v in b.r.items():
                eng.wait(s, v)
        inst = fn()
        s, v = eng.signal(inst, signal)
        for b in reads:
            if b.r.get(s, 0) < v:
                b.r[s] = v
        for b in writes:
            b.w = {s: v}
            b.r = {}
        return inst

    def dma(self, q, pairs, sb, reads=(), writes=(), **kw):
        if sb.sem is None:
            sb.sem = self.dma_sem()
        sem = sb.sem
        q.wait(sem, sem.count)
        for b in reads:
            for s, v in b.w.items():
                q.wait(s, v)
        for b in writes:
            for s, v in b.w.items():
                q.wait(s, v)
            for s, v in b.r.items():
                q.wait(s, v)
        for (o, i) in pairs:
            q.h.dma_start(out=o, in_=i, **kw).then_inc(sem.h, 16)
            sem.count += 16
        v = sem.count
        for b in reads:
            if b.r.get(sem, 0) < v:
                b.r[sem] = v
        for b in writes:
            b.w = {sem: v}
            b.r = {}

    def barrier(self):
        sems = []
        for e in self.engs:
            assert not e.pending, f"pending unsignalled instruction on {e.name}"
            for s in e.own:
                sems.append(s)
        sems += self.dma_sems
        for e in self.engs:
            for s in sems:
                e.wait(s, s.count)


def _bf16(a):
    return np.asarray(a, dtype=np.float32).astype(ml_dtypes.bfloat16)


def _consts(LS, LP):
    c = {}
    c["ident_f"] = np.eye(128, dtype=np.float32)
    pm = np.zeros((128, 128), np.float32)
    for p in range(128):
        d = p % 32
        partner = p + 16 if d < 16 else p - 16
        pm[partner, p] = 1.0
    c["pm_b"] = _bf16(pm)
    t = np.arange(LS)
    row = (t // 64).astype(np.float64)
    col = (t % 64).astype(np.float64)
    inv = 10000.0 ** (-np.arange(16, dtype=np.float64) / 16.0)
    cos = np.zeros((128, LS), np.float64)
    sin = np.zeros((128, LS), np.float64)
    for p in range(128):
        d = p % 64
        i = d % 16
        pos = row if d < 32 else col
        ang = pos.astype(np.float32).astype(np.float64) * np.float32(inv[i]).astype(np.float64)
        cos[p] = np.cos(ang)
        sgn = -1.0 if (d % 32) < 16 else 1.0
        sin[p] = sgn * np.sin(ang)
    c["rope_cos"] = cos.astype(np.float32)
    c["rope_sin"] = sin.astype(np.float32)

    def dft(L, nb):
        l = np.arange(L, dtype=np.int64)
        kl = (l[:, None] * l[None, :]) % L
        ang = 2.0 * np.pi * kl.astype(np.float64) / L
        C = np.cos(ang)
        S = -np.sin(ang)
        nlc = L // 128
        kb = L // nb
        def lay(M):
            M4 = M.reshape(nlc, 128, nb, kb).transpose(2, 1, 0, 3)
            return _bf16(np.ascontiguousarray(M4))
        return lay(C), lay(S)
    fc_, fs_ = dft(LS, LS // 512)
    nh = max(1, (LS // 512) // 2)
    c["dfts_c"], c["dfts_s"] = np.ascontiguousarray(fc_[:nh]), np.ascontiguousarray(fs_[:nh])
    alt = np.ones((128, 2), np.float32)
    alt[1::2, :] = -1.0
    c["alt_b"] = _bf16(alt)
    c["dftp_c"], c["dftp_s"] = dft(LP, 1)
    a = 2.0 * np.pi * np.outer(np.arange(64), np.arange(64)) / 64.0
    z = np.zeros((128, 128), np.float32)
    cb = z.copy(); sbd = z.copy()
    for hf in range(2):
        cb[hf * 64:(hf + 1) * 64, hf * 64:(hf + 1) * 64] = np.cos(a)
        sbd[hf * 64:(hf + 1) * 64, hf * 64:(hf + 1) * 64] = np.sin(a)
    c["c64"] = cb
    c["s64"] = sbd
    return c


class StopBuild(Exception):
    pass


def build_program(depth=DEPTH, LS=4096, LP=256, NPR=2, stop=None):
    nc = bass.Bass("TRN2", target_bir_lowering=False)

    def din(name, shape, dt=F32):
        return nc.dram_tensor(name, list(shape), dt, kind="ExternalInput").ap()

    def dout(name, shape, dt=F32):
        return nc.dram_tensor(name, list(shape), dt, kind="ExternalOutput").ap()

    def dscr(name, shape, dt):
        return nc.dram_tensor(name, list(shape), dt, kind="Internal").ap()

    xs_in = din("xs", [LS, D])
    xp_in = din("xp", [NPR, LP, D])
    c2 = din("c2", [2, D])
    ck = din("ck", [DEPTH, H, NPAST, 128])
    cv = din("cv", [DEPTH, H, NPAST, 128])
    w_ada = din("w_ada", [DEPTH, D, 3 * D])
    b_ada = din("b_ada", [DEPTH, 3 * D])
    w_in = din("w_in", [DEPTH, D, 3328])
    w_out = din("w_out", [DEPTH, D, D])
    lam4 = din("lam4", [DEPTH, 256])
    subln = din("subln_w", [DEPTH, 128])
    four_w = din("fourier_w", [DEPTH, 4, 64, 64])
    g_ws = din("gmlp_ws", [DEPTH, 4, 128, 128])
    g_bs = din("gmlp_bs", [DEPTH, 4, 128])
    ln_g = din("ln_g", [DEPTH, D])
    ln_b = din("ln_b", [DEPTH, D])
    ident_f = din("ident_f", [128, 128])
    pm_b = din("pm_b", [128, 128], BF16)
    rope_cos = din("rope_cos", [128, LS])
    rope_sin = din("rope_sin", [128, LS])
    NKB_S = LS // 512
    NKH = max(1, NKB_S // 2)
    dfts_c = din("dfts_c", [NKH, 128, LS // 128, 512], BF16)
    dfts_s = din("dfts_s", [NKH, 128, LS // 128, 512], BF16)
    alt_b = din("alt_b", [128, 2], BF16)
    dftp_c = din("dftp_c", [1, 128, LP // 128, LP], BF16)
    dftp_s = din("dftp_s", [1, 128, LP // 128, LP], BF16)
    c64 = din("c64", [128, 128])
    s64 = din("s64", [128, 128])

    y_s = dout("y_s", [LS, D])
    y_p = dout("y_p", [NPR, LP, D])
    nk_o = dout("nk", [NPR, DEPTH, H, LP, 128])
    nv_o = dout("nv", [NPR, DEPTH, H, LP, 128])

    class Job:
        pass
    jobs = []
    for ji in range(1 + NPR):
        J = Job()
        J.idx = ji
        J.sample = (ji == 0)
        J.L = LS if J.sample else LP
        J.npast = NPAST if J.sample else 0
        J.Lk = J.L + J.npast
        J.modrow = 0 if J.sample else 1
        J.x_in = xs_in if J.sample else xp_in[ji - 1]
        J.y_out = y_s if J.sample else y_p[ji - 1]
        J.xscr = [dscr(f"xscr{ji}_{i}", [J.L, D], F32) for i in range(2)]
        J.QT = dscr(f"qt{ji}", [H, 128, J.L], BF16)
        J.KT = dscr(f"kt{ji}", [H, 128, J.Lk], BF16)
        J.V = dscr(f"v{ji}", [J.Lk, 512], BF16)
        J.PQ = dscr(f"pq{ji}", [J.L, 512], BF16)
        J.SGA = dscr(f"sga{ji}", [J.L, 512], BF16)
        J.SGF = dscr(f"sgf{ji}", [256, J.L], BF16)
        J.MIXT = dscr(f"mixt{ji}", [D, J.L], BF16)
        J.NQ = 512 if J.sample else 256
        J.nblk = J.L // J.NQ
        J.tpb = J.NQ // 128
        J.b_x = {}
        J.b_qk = [Buf(f"qk{ji}_{b}") for b in range(J.nblk)]
        J.b_kpast = Buf(f"kpast{ji}")
        J.b_v = [Buf(f"v{ji}_{b}") for b in range(J.nblk)]
        J.b_vpast = Buf(f"vpast{ji}")
        J.b_pq = [Buf(f"pq{ji}_{b}") for b in range(J.nblk)]
        J.b_sga = [Buf(f"sga{ji}_{b}") for b in range(J.nblk)]
        J.b_sgf = [Buf(f"sgf{ji}_{b}") for b in range(J.nblk)]
        J.b_mix = [Buf(f"mix{ji}_{b}") for b in range(J.nblk)]
        J.b_xs = [[Buf(f"xs{ji}_{i}_{t}") for t in range(J.L // 128)] for i in range(2)]
        jobs.append(J)

    mod_d = dscr("mod_d", [DEPTH, 2, 3 * D], F32)
    b_mod = Buf("mod_d")

    with ExitStack() as gs:
        fw = FW(nc, gs)
        PE, ACT, DVE, POOL, SP = fw.pe, fw.act, fw.dve, fw.pool_e, fw.sp
        block = gs.enter_context(nc.Block())

        def sb(stack, shape, dt, name="t"):
            t = stack.enter_context(nc.sbuf_tensor(fw.name(name), list(shape), dt))
            return t, Buf(name)

        ps = gs.enter_context(nc.psum_tensor(fw.name("ps"), [128, 4096], F32))
        bankb = [Buf(f"bank{i}") for i in range(8)]

        def bank(i, w=512):
            return ps[:, i * 512:i * 512 + w]

        identf, b_identf = sb(gs, [128, 128], F32, "identf")
        pmb, b_pmb = sb(gs, [128, 128], BF16, "pmb")
        altt, b_alt = sb(gs, [128, 2], BF16, "alt")
        WF, b_WF = sb(gs, [128, 8, 1280], BF16, "WF")
        WT, b_WT = sb(gs, [128, 8, 2560], BF16, "WT")
        WO, b_WO = sb(gs, [128, 8, 1024], BF16, "WO")
        Gt = [sb(gs, [128, 1024], F32, "G") for _ in range(2)]
        lnG, b_lnG = sb(gs, [128, 1024], F32, "lnG")
        lnB, b_lnB = sb(gs, [128, 1024], F32, "lnB")
        ssm = [sb(gs, [128, 2, 8], F32, "ssm") for _ in range(2)]
        lamt, b_lamt = sb(gs, [128, 256], F32, "lamt")
        lamv, b_lamv = sb(gs, [128, 8], F32, "lamv")
        SW, b_SW = sb(gs, [128, 128], F32, "SW")
        wsT, b_wsT = sb(gs, [128, 4, 128], BF16, "wsT")
        bsT, b_bsT = sb(gs, [128, 4], F32, "bsT")
        AB, b_AB = sb(gs, [128, 2, 256], BF16, "AB")
        nhalf, b_nh = sb(gs, [128, 16], F32, "nhalf")

        fw.dma(SP, [(identf[:], ident_f)], b_identf, writes=[b_identf])
        fw.dma(SP, [(pmb[:], pm_b)], b_pmb, writes=[b_pmb])
        fw.dma(SP, [(altt[:], alt_b)], b_alt, writes=[b_alt])
        fw.op(DVE, lambda: nc.vector.memset(nhalf[:], -0.5), writes=[b_nh])

        def chk(name):
            if stop == name:
                raise StopBuild()

        try:
            with ExitStack() as ph:
                cT, b_cT = sb(ph, [128, 8, 2], F32, "cT")
                cTs, b_cTs = sb(ph, [128, 8, 2], F32, "cTs")
                wa = [sb(ph, [128, 8, 512], F32, "wa") for _ in range(2)]
                bada, b_bada = sb(ph, [2, 3 * D], F32, "bada")
                modr, b_modr = sb(ph, [2, 3 * D], F32, "modr")
                fw.dma(SP, [(cT[:, :, r], c2[r].rearrange("(j p) -> p j", p=128)) for r in range(2)], b_cT,
                       writes=[b_cT], allow_slow_non_contiguous=True)
                fw.op(ACT, lambda: nc.scalar.activation(out=cTs[:], in_=cT[:], func=AF.Silu),
                      reads=[b_cT], writes=[b_cTs])
                it = 0
                for l in range(depth):
                    fw.dma(SP, [(bada[0:1, :], b_ada[l:l + 1, :]), (bada[1:2, :], b_ada[l:l + 1, :])],
                           b_bada, writes=[b_bada])
                    for cb in range(6):
                        wt_, wb_ = wa[it % 2]
                        src = w_ada[l][:, cb * 512:(cb + 1) * 512].rearrange("(j p) n -> p j n", p=128)
                        fw.dma(SP, [(wt_[:, 0:4, :], src[:, 0:4, :]), (wt_[:, 4:8, :], src[:, 4:8, :])],
                               wb_, writes=[wb_])
                        bi = it % 2
                        for j in range(8):
                            fw.op(PE, (lambda j=j, wt_=wt_, bi=bi: nc.tensor.matmul(
                                bank(bi)[0:2, :], lhsT=cTs[:, j, :], rhs=wt_[:, j, :],
                                start=(j == 0), stop=(j == 7))),
                                reads=[b_cTs, wb_], writes=[bankb[bi]], signal=(j == 7))
                        fw.op(DVE, (lambda cb=cb, bi=bi: nc.vector.tensor_tensor(
                            out=modr[:, cb * 512:(cb + 1) * 512], in0=bank(bi)[0:2, :],
                            in1=bada[:, cb * 512:(cb + 1) * 512], op=ALU.add)),
                            reads=[bankb[bi], b_bada], writes=[b_modr])
                        it += 1
                    fw.dma(SP, [(mod_d[l], modr[:])], b_modr, reads=[b_modr], writes=[b_mod])
                fw.barrier()
                fw.release([b_cT, b_cTs, wa[0][1], wa[1][1], b_bada, b_modr])
            chk('prologue')

            cast_rr = [0]

            def load_cast(dst_ap_fn, src_ap, dst_buf, ncols):
                st, sbf = wst[cast_rr[0] % len(wst)]
                cast_rr[0] += 1
                s3 = src_ap.rearrange("(j p) n -> p j n", p=128)
                fw.dma(SP, [(st[:, 0:4, :ncols], s3[:, 0:4, :]), (st[:, 4:8, :ncols], s3[:, 4:8, :])],
                       sbf, writes=[sbf])
                if cast_rr[0] % 2 == 0:
                    fw.op(POOL, lambda: nc.gpsimd.tensor_copy(out=dst_ap_fn(), in_=st[:, :, :ncols]),
                          reads=[sbf], writes=[dst_buf])
                else:
                    fw.op(ACT, lambda: nc.scalar.copy(out=dst_ap_fn(), in_=st[:, :, :ncols]),
                          reads=[sbf], writes=[dst_buf])

            bank_rr = [2]

            def next_bank():
                b = bank_rr[0]
                bank_rr[0] = 2 + (bank_rr[0] - 2 + 1) % 6
                return b

            for l in range(depth):
                lam_init = 0.8 - 0.6 * math.exp(-0.3 * l)

                wi = w_in[l]
                wph = ExitStack()
                wst = [sb(wph, [128, 8, 256], F32, "wst") for _ in range(6)]
                for c0 in range(0, 1024, 256):
                    load_cast(lambda c0=c0: WF[:, :, c0:c0 + 256], wi[:, c0:c0 + 256], b_WF, 256)
                load_cast(lambda: WF[:, :, 1024:1280], wi[:, 2304:2560], b_WF, 256)
                for i, c0 in enumerate(range(1024, 1536, 256)):
                    load_cast(lambda i=i: WT[:, :, i * 256:(i + 1) * 256], wi[:, c0:c0 + 256], b_WT, 256)
                for i, c0 in enumerate(range(2560, 3072, 256)):
                    load_cast(lambda i=i: WT[:, :, 1024 + i * 256:1024 + (i + 1) * 256], wi[:, c0:c0 + 256], b_WT, 256)
                load_cast(lambda: WT[:, :, 1536:1792], wi[:, 3072:3328], b_WT, 256)
                for i, c0 in enumerate(range(1536, 2048, 256)):
                    load_cast(lambda i=i: WT[:, :, 2048 + i * 256:2048 + (i + 1) * 256], wi[:, c0:c0 + 256], b_WT, 256)
                load_cast(lambda: WT[:, :, 1792:2048], wi[:, 2048:2304], b_WT, 256)
                for c0 in range(0, 1024, 256):
                    load_cast(lambda c0=c0: WO[:, :, c0:c0 + 256], w_out[l][:, c0:c0 + 256], b_WO, 256)
                fw.barrier()
                fw.release([b_ for (_, b_) in wst])
                wph.close()
                chk('weights')

                with ExitStack() as ph:
                    for r in range(2):
                        t_, b_ = Gt[r]
                        fw.dma(SP, [(t_[:], mod_d[l, r, 2 * D:3 * D].partition_broadcast(128))], b_, reads=[b_mod], writes=[b_])
                        s_, sb_ = ssm[r]
                        fw.dma(SP, [(s_[:], mod_d[l, r, 0:2 * D].rearrange("(s j p) -> p s j", p=128, j=8))],
                               sb_, reads=[b_mod], writes=[sb_], allow_slow_non_contiguous=True)
                        fw.op(DVE, (lambda s_=s_: nc.vector.tensor_scalar(
                            out=s_[:, 1, :], in0=s_[:, 1, :], scalar1=1.0, scalar2=None, op0=ALU.add)),
                            reads=[sb_], writes=[sb_])
                    fw.dma(SP, [(lnG[:], ln_g[l].partition_broadcast(128))], b_lnG, writes=[b_lnG])
                    fw.dma(SP, [(lnB[:], ln_b[l].partition_broadcast(128))], b_lnB, writes=[b_lnB])
                    fw.dma(SP, [(SW[:], subln[l].partition_broadcast(128))], b_SW, writes=[b_SW])
                    fw.op(DVE, lambda: nc.vector.tensor_scalar(out=SW[:], in0=SW[:], scalar1=(1.0 - lam_init), scalar2=None,
                                                               op0=ALU.mult), reads=[b_SW], writes=[b_SW])
                    fw.dma(SP, [(lamt[:], lam4[l].partition_broadcast(128))], b_lamt, writes=[b_lamt])
                    junk, b_junk = sb(ph, [128, 64], F32, "junk")
                    for i in range(2):
                        fw.op(DVE, (lambda i=i: nc.vector.tensor_tensor(
                            out=junk[:], in0=lamt[:, i * 128:i * 128 + 64],
                            in1=lamt[:, i * 128 + 64:i * 128 + 128], op=ALU.mult)),
                            reads=[b_lamt], writes=[b_junk])
                        fw.op(DVE, (lambda i=i: nc.vector.reduce_sum(
                            out=lamv[:, i:i + 1], in_=junk[:], axis=mybir.AxisListType.X)),
                            reads=[b_junk], writes=[b_lamv])
                    fw.op(ACT, lambda: nc.scalar.activation(out=lamv[:, 2:4], in_=lamv[:, 0:2], func=AF.Exp),
                          reads=[b_lamv], writes=[b_lamv])
                    fw.op(DVE, lambda: nc.vector.scalar_tensor_tensor(
                        out=lamv[:, 4:5], in0=lamv[:, 2:3], scalar=lam_init, in1=lamv[:, 3:4],
                        op0=ALU.add, op1=ALU.subtract), reads=[b_lamv], writes=[b_lamv])
                    fw.op(DVE, lambda: nc.vector.tensor_scalar(
                        out=lamv[:, 5:6], in0=lamv[:, 4:5], scalar1=-1.0, scalar2=None, op0=ALU.mult),
                        reads=[b_lamv], writes=[b_lamv])
                    wsf, b_wsf = sb(ph, [128, 4, 128], F32, "wsf")
                    fw.dma(SP, [(wsf[:], g_ws[l].rearrange("h p q -> p h q"))], b_wsf, writes=[b_wsf])
                    fw.dma(SP, [(bsT[:], g_bs[l].rearrange("h p -> p h"))], b_bsT, writes=[b_bsT],
                           allow_slow_non_contiguous=True)
                    for hh in range(4):
                        bi = next_bank()
                        fw.op(PE, (lambda hh=hh, bi=bi: nc.tensor.transpose(
                            out=bank(bi, 128), in_=wsf[:, hh, :], identity=identf[:])),
                            reads=[b_wsf, b_identf], writes=[bankb[bi]])
                        fw.op(DVE, (lambda hh=hh, bi=bi: nc.vector.tensor_copy(out=wsT[:, hh, :], in_=bank(bi, 128))),
                              reads=[bankb[bi]], writes=[b_wsT])
                    cs, b_cs = sb(ph, [128, 2, 128], F32, "cs")
                    fwt, b_fwt = sb(ph, [128, 2, 64], F32, "fwt")
                    fw.dma(SP, [(cs[:, 0, :], c64), (cs[:, 1, :], s64)], b_cs, writes=[b_cs])
                    fw.dma(SP, [(fwt[:, pr, :], four_w[l, 2 * pr:2 * pr + 2].rearrange("g c e -> (g c) e"))
                                for pr in range(2)], b_fwt, writes=[b_fwt])
                    fw.op(DVE, lambda: nc.vector.memset(AB[:], 0.0), writes=[b_AB])
                    for pr in range(2):
                        bi = next_bank()
                        fw.op(PE, (lambda pr=pr, bi=bi: nc.tensor.matmul(
                            bank(bi)[:, 0:64], lhsT=cs[:, 0, :], rhs=fwt[:, pr, :], start=True, stop=True)),
                            reads=[b_cs, b_fwt], writes=[bankb[bi]], signal=False)
                        fw.op(PE, (lambda pr=pr, bi=bi: nc.tensor.matmul(
                            bank(bi)[:, 64:128], lhsT=cs[:, 1, :], rhs=fwt[:, pr, :], start=True, stop=True,
                            skip_group_check=True)),
                            reads=[b_cs, b_fwt], writes=[bankb[bi]])
                        for hf in range(2):
                            rs = slice(hf * 64, hf * 64 + 64)
                            fw.op(DVE, (lambda rs=rs, pr=pr, bi=bi, hf=hf: nc.vector.tensor_copy(
                                out=AB[rs, pr, hf * 128:(hf + 1) * 128], in_=bank(bi)[rs, 0:128])),
                                reads=[bankb[bi]], writes=[b_AB])
                    wff, b_wff = sb(ph, [128, 8, 256], F32, "wff")
                    wfT, b_wfT = sb(ph, [128, 2, 1024], BF16, "wfT")
                    fw.op(DVE, lambda: nc.vector.tensor_copy(out=wff[:], in_=WT[:, :, 1792:2048]),
                          reads=[b_WT], writes=[b_wff])
                    for j in range(8):
                        for pr in range(2):
                            bi = next_bank()
                            fw.op(PE, (lambda j=j, pr=pr, bi=bi: nc.tensor.transpose(
                                out=bank(bi, 128), in_=wff[:, j, pr * 128:(pr + 1) * 128], identity=identf[:])),
                                reads=[b_wff, b_identf], writes=[bankb[bi]])
                            fw.op(DVE, (lambda j=j, pr=pr, bi=bi: nc.vector.tensor_copy(
                                out=wfT[:, pr, j * 128:(j + 1) * 128], in_=bank(bi, 128))),
                                reads=[bankb[bi]], writes=[b_wfT])
                    for j in range(8):
                        bi = next_bank()
                        for pr in range(2):
                            fw.op(PE, (lambda j=j, pr=pr, bi=bi: nc.tensor.matmul(
                                bank(bi)[:, pr * 256:(pr + 1) * 256], lhsT=wfT[:, pr, j * 128:(j + 1) * 128],
                                rhs=AB[:, pr, :], start=True, stop=True, skip_group_check=True)),
                                reads=[b_wfT, b_AB], writes=[bankb[bi]], signal=(pr == 1))
                        src = bank(bi).rearrange("p (g t e) -> p g t e", g=4, t=2)
                        fw.op(DVE, (lambda j=j, src=src: nc.vector.tensor_copy(
                            out=WT[:, j, 512:768].rearrange("p (g e) -> p g e", g=4), in_=src[:, :, 0, :])),
                            reads=[bankb[bi]], writes=[b_WT])
                        fw.op(DVE, (lambda j=j, src=src: nc.vector.tensor_copy(
                            out=WT[:, j, 768:1024].rearrange("p (g e) -> p g e", g=4), in_=src[:, :, 1, :])),
                            reads=[bankb[bi]], writes=[b_WT])
                    fw.barrier()
                    fw.release([b_junk, b_wsf, b_cs, b_fwt, b_wff, b_wfT])
                chk('setup')

                src_i = (l - 1) % 2
                dst_i = l % 2

                def x_src(J, t):
                    if l == 0:
                        return J.x_in[t * 128:(t + 1) * 128, :], []
                    return J.xscr[src_i][t * 128:(t + 1) * 128, :], [J.b_xs[src_i][t]]

                def x_dst(J, t):
                    if l == depth - 1:
                        return J.y_out[t * 128:(t + 1) * 128, :], []
                    return J.xscr[dst_i][t * 128:(t + 1) * 128, :], [J.b_xs[dst_i][t]]

                for J in jobs:
                    with ExitStack() as ph:
                        NQ, tpb = J.NQ, J.tpb
                        ss_t, ss_b = ssm[J.modrow]
                        xt = [sb(ph, [128, 1024], F32, "xt") for _ in range(tpb)]
                        xn = [sb(ph, [128, 1024], F32, "xn") for _ in range(tpb)]
                        st6, b_st6 = sb(ph, [128, 2, 6], F32, "st6")
                        mv4, b_mv4 = sb(ph, [128, tpb, 2], F32, "mv4")
                        ve4, b_ve4 = sb(ph, [128, tpb], F32, "ve4")
                        rs4, b_rs4 = sb(ph, [128, tpb], F32, "rs4")
                        hT = [sb(ph, [128, 8, NQ], BF16, "hT") for _ in range(2)]
                        rc = [sb(ph, [128, 2, NQ], F32, "rc") for _ in range(1)] if J.sample else None
                        qraw = [sb(ph, [128, NQ], BF16, "qraw") for _ in range(2)]
                        t1 = [sb(ph, [128, NQ], F32, "t1") for _ in range(1)]
                        t2 = [sb(ph, [128, NQ], F32, "t2") for _ in range(1)]
                        fst = [sb(ph, [128, NQ], BF16, "fst") for _ in range(3)]
                        vst = [sb(ph, [128, tpb, 512], BF16, "vst") for _ in range(1)]
                        pqst = [sb(ph, [128, tpb, 512], BF16, "pqst") for _ in range(1)]
                        gast = [sb(ph, [128, tpb, 512], BF16, "gast") for _ in range(1)]
                        mxst = [sb(ph, [128, 2, NQ], BF16, "mxst") for _ in range(2)]
                        uv4s = [sb(ph, [128, tpb, 512], F32, "uv") for _ in range(2)]
                        sgm4s = [sb(ph, [128, tpb, 256], F32, "sgm") for _ in range(2)]
                        gst, b_gst = sb(ph, [128, 4, 6], F32, "gst")
                        gmvs = [sb(ph, [128, tpb * 4, 2], F32, "gmv") for _ in range(2)]
                        gve, b_gve = sb(ph, [128, tpb * 4], F32, "gve")
                        grss = [sb(ph, [128, tpb * 4], F32, "grs") for _ in range(2)]
                        vnbs = [sb(ph, [128, 256], BF16, "vnb") for _ in range(tpb)]
                        gmos = [sb(ph, [128, 256], F32, "gmo") for _ in range(tpb)]
                        kvst = [sb(ph, [128, 512], F32, "kvst") for _ in range(2)] if not J.sample else None
                        phase_bufs = [b_st6, b_mv4, b_ve4, b_rs4, b_gst, b_gve] + [b for lst in (uv4s, sgm4s, gmvs, grss, vnbs, gmos) for (_, b) in lst]
                        for lst in (xt, xn, hT, qraw, t1, t2, fst, vst, pqst, gast, mxst):
                            phase_bufs += [b for (_, b) in lst]
                        if rc:
                            phase_bufs += [b for (_, b) in rc]
                        if kvst:
                            phase_bufs += [b for (_, b) in kvst]
                        kv_rr = [0]
                        f_rr = [0]

                        if J.sample:
                            ckt = [sb(ph, [128, 2, 128], F32, "ckt") for _ in range(2)]
                            phase_bufs += [b for (_, b) in ckt]
                            for hh in range(H):
                                ct, cb_ = ckt[0]
                                fw.dma(SP, [(ct[:], ck[l, hh].rearrange("(t p) e -> p t e", p=128))], cb_, writes=[cb_])
                                ft, fb = fst[f_rr[0] % 3]; f_rr[0] += 1
                                for tt in range(2):
                                    bi = next_bank()
                                    fw.op(PE, (lambda ct=ct, tt=tt, bi=bi: nc.tensor.transpose(
                                        out=bank(bi, 128), in_=ct[:, tt, :], identity=identf[:])),
                                        reads=[cb_, b_identf], writes=[bankb[bi]])
                                    fw.op(ACT, (lambda ft=ft, tt=tt, bi=bi: nc.scalar.copy(
                                        out=ft[:, tt * 128:(tt + 1) * 128], in_=bank(bi, 128))),
                                        reads=[bankb[bi]], writes=[fb])
                                fw.dma(POOL, [(J.KT[hh][:, 0:NPAST], ft[:, 0:NPAST])], fb, reads=[fb], writes=[J.b_kpast])
                                ct, cb_ = ckt[1]
                                fw.dma(SP, [(ct[:], cv[l, hh].rearrange("(t p) e -> p t e", p=128))], cb_, writes=[cb_])
                                vt, vb = vst[0]
                                fw.op(DVE, (lambda ct=ct, vt=vt, hh=hh: nc.vector.tensor_copy(
                                    out=vt[:, 0:2, hh * 128:(hh + 1) * 128], in_=ct[:])),
                                    reads=[cb_], writes=[vb])
                            vt, vb = vst[0]
                            fw.dma(POOL, [(J.V[0:NPAST, :].rearrange("(t p) c -> p t c", p=128), vt[:, 0:2, :])],
                                   vb, reads=[vb], writes=[J.b_vpast])

                        def ln_stats(blk):
                            for tt in range(tpb):
                                t = blk * tpb + tt
                                x_t, x_b = xt[tt]
                                src, sdeps = x_src(J, t)
                                fw.dma(SP, [(x_t[:, 0:512], src[:, 0:512]), (x_t[:, 512:1024], src[:, 512:1024])],
                                       x_b, reads=sdeps, writes=[x_b])
                                for hf in range(2):
                                    fw.op(DVE, (lambda hf=hf, x_t=x_t: nc.vector.bn_stats(
                                        out=st6[:, hf, :], in_=x_t[:, hf * 512:(hf + 1) * 512])),
                                        reads=[x_b], writes=[b_st6])
                                fw.op(DVE, lambda: nc.vector.bn_aggr(out=mv4[:, tt, :], in_=st6[:].rearrange("p a b -> p (a b)")),
                                      reads=[b_st6], writes=[b_mv4])
                            fw.op(DVE, lambda: nc.vector.tensor_scalar(
                                out=ve4[:], in0=mv4[:, :, 1], scalar1=LN_EPS, scalar2=None, op0=ALU.add),
                                reads=[b_mv4], writes=[b_ve4])
                            fw.op(POOL, lambda: nc.gpsimd.tensor_tensor(out=rs4[:], in0=ve4[:], in1=nhalf[:, 0:tpb], op=ALU.pow),
                                  reads=[b_ve4, b_nh], writes=[b_rs4])

                            for tt in range(tpb):
                                x_t, x_b = xt[tt]
                                xn_t, xn_b = xn[tt]
                                fw.op(DVE, (lambda x_t=x_t, xn_t=xn_t: nc.vector.tensor_scalar(
                                    out=xn_t[:], in0=x_t[:], scalar1=mv4[:, tt, 0:1], scalar2=rs4[:, tt:tt + 1],
                                    op0=ALU.subtract, op1=ALU.mult)), reads=[x_b, b_mv4, b_rs4], writes=[xn_b])

                        def ln_apply(blk):
                            hT_t, hT_b = hT[blk % 2]
                            for tt in range(tpb):
                                xn_t, xn_b = xn[tt]
                                for half in range(2):
                                    for jj in range(4):
                                        j = half * 4 + jj
                                        fw.op(PE, (lambda j=j, jj=jj, half=half, xn_t=xn_t: nc.tensor.transpose(
                                            out=ps[:, half * 512 + jj * 128: half * 512 + (jj + 1) * 128],
                                            in_=xn_t[:, j * 128:(j + 1) * 128], identity=identf[:])),
                                            reads=[xn_b, b_identf], writes=[bankb[half]], signal=(jj == 3))
                                    for jj in range(4):
                                        j = half * 4 + jj
                                        if jj % 2 == 0:
                                            fw.op(ACT, (lambda j=j, jj=jj, half=half, tt=tt, hT_t=hT_t: nc.scalar.activation(
                                                out=hT_t[:, j, tt * 128:(tt + 1) * 128],
                                                in_=ps[:, half * 512 + jj * 128: half * 512 + (jj + 1) * 128],
                                                func=AF.Identity, bias=ss_t[:, 0, j:j + 1], scale=ss_t[:, 1, j:j + 1])),
                                                reads=[bankb[half], ss_b], writes=[hT_b])
                                        else:
                                            fw.op(DVE, (lambda j=j, jj=jj, half=half, tt=tt, hT_t=hT_t: nc.vector.tensor_scalar(
                                                out=hT_t[:, j, tt * 128:(tt + 1) * 128],
                                                in0=ps[:, half * 512 + jj * 128: half * 512 + (jj + 1) * 128],
                                                scalar1=ss_t[:, 1, j:j + 1], scalar2=ss_t[:, 0, j:j + 1],
                                                op0=ALU.mult, op1=ALU.add)),
                                                reads=[bankb[half], ss_b], writes=[hT_b])

                        def fm(blk):
                            hT_t, hT_b = hT[blk % 2]
                            if J.sample:
                                rc_t, rc_b = rc[0]
                                fw.dma(SP, [(rc_t[:, 0, :], rope_cos[:, blk * NQ:(blk + 1) * NQ]),
                                            (rc_t[:, 1, :], rope_sin[:, blk * NQ:(blk + 1) * NQ])], rc_b, writes=[rc_b])
                            pendr = []

                            def rope_finish():
                                (fc_, hh_, dstT_, c_off_, ft_, fb_, qr_t, qr_b) = pendr.pop(0)
                                t1_t, t1_b = t1[0]
                                t2_t, t2_b = t2[0]
                                b2 = next_bank()
                                fw.op(PE, lambda: nc.tensor.matmul(bank(b2, NQ), lhsT=pmb[:], rhs=qr_t[:], start=True, stop=True),
                                      reads=[b_pmb, qr_b], writes=[bankb[b2]])
                                fw.op(DVE, lambda: nc.vector.tensor_tensor(out=t1_t[:], in0=qr_t[:], in1=rc_t[:, 0, :], op=ALU.mult),
                                      reads=[qr_b, rc_b], writes=[t1_b])
                                fw.op(DVE, lambda: nc.vector.tensor_tensor(out=t2_t[:], in0=bank(b2, NQ), in1=rc_t[:, 1, :], op=ALU.mult),
                                      reads=[bankb[b2], rc_b], writes=[t2_b])
                                fw.op(DVE, lambda: nc.vector.tensor_tensor(out=ft_[:], in0=t1_t[:], in1=t2_t[:], op=ALU.add),
                                      reads=[t1_b, t2_b], writes=[fb_])
                                fw.dma(POOL, [(dstT_[hh_][:, c_off_:c_off_ + NQ], ft_[:])], fb_, reads=[fb_],
                                       writes=[J.b_qk[blk]])

                            for fc in [8, 9] + list(range(8)):
                                bi = next_bank()
                                for j in range(8):
                                    fw.op(PE, (lambda j=j, fc=fc, bi=bi, hT_t=hT_t: nc.tensor.matmul(
                                        bank(bi, NQ), lhsT=WF[:, j, fc * 128:(fc + 1) * 128], rhs=hT_t[:, j, :],
                                        start=(j == 0), stop=(j == 7))),
                                        reads=[b_WF, hT_b], writes=[bankb[bi]], signal=(j == 7))
                                if pendr:
                                    rope_finish()
                                if fc < 8:
                                    hh = fc % 4
                                    dstT = J.QT if fc < 4 else J.KT
                                    c_off = blk * NQ + (0 if fc < 4 else J.npast)
                                    ft, fb = fst[f_rr[0] % 3]; f_rr[0] += 1
                                    if J.sample:
                                        qr_t, qr_b = qraw[fc % 2]
                                        t1_t, t1_b = t1[0]
                                        t2_t, t2_b = t2[0]
                                        fw.op(ACT, (lambda qr_t=qr_t, bi=bi: nc.scalar.copy(out=qr_t[:], in_=bank(bi, NQ))),
                                              reads=[bankb[bi]], writes=[qr_b])
                                        pendr.append((fc, hh, dstT, c_off, ft, fb, qr_t, qr_b))
                                        continue
                                        b2 = next_bank()
                                        fw.op(PE, (lambda qr_t=qr_t, b2=b2: nc.tensor.matmul(
                                            bank(b2, NQ), lhsT=pmb[:], rhs=qr_t[:], start=True, stop=True)),
                                            reads=[b_pmb, qr_b], writes=[bankb[b2]])
                                        fw.op(DVE, (lambda qr_t=qr_t, t1_t=t1_t, rc_t=rc_t: nc.vector.tensor_tensor(
                                            out=t1_t[:], in0=qr_t[:], in1=rc_t[:, 0, :], op=ALU.mult)),
                                            reads=[qr_b, rc_b], writes=[t1_b])
                                        fw.op(DVE, (lambda t2_t=t2_t, b2=b2, rc_t=rc_t: nc.vector.tensor_tensor(
                                            out=t2_t[:], in0=bank(b2, NQ), in1=rc_t[:, 1, :], op=ALU.mult)),
                                            reads=[bankb[b2], rc_b], writes=[t2_b])
                                        fw.op(DVE, (lambda ft=ft, t1_t=t1_t, t2_t=t2_t: nc.vector.tensor_tensor(
                                            out=ft[:], in0=t1_t[:], in1=t2_t[:], op=ALU.add)),
                                            reads=[t1_b, t2_b], writes=[fb])
                                    else:
                                        fw.op(ACT, (lambda ft=ft, bi=bi: nc.scalar.copy(out=ft[:], in_=bank(bi, NQ))),
                                              reads=[bankb[bi]], writes=[fb])
                                    fw.dma(POOL, [(dstT[hh][:, c_off:c_off + NQ], ft[:])], fb, reads=[fb],
                                           writes=[J.b_qk[blk]])
                                else:
                                    ft, fb = fst[f_rr[0] % 3]; f_rr[0] += 1
                                    fw.op(ACT, (lambda ft=ft, bi=bi: nc.scalar.activation(
                                        out=ft[:], in_=bank(bi, NQ), func=AF.Silu)),
                                        reads=[bankb[bi]], writes=[fb])
                                    cc = fc - 8
                                    fw.dma(POOL, [(J.SGF[cc * 128:(cc + 1) * 128, blk * NQ:(blk + 1) * NQ], ft[:])], fb,
                                           reads=[fb], writes=[J.b_sgf[blk]])
                            while pendr:
                                rope_finish()

                        def tmaj(blk):
                            hT_t, hT_b = hT[blk % 2]
                            uv4, b_uv = uv4s[blk % 2]
                            sgm4, b_sgm = sgm4s[blk % 2]
                            gmv, b_gmv = gmvs[blk % 2]
                            grs, b_grs = grss[blk % 2]
                            pb = blk - 1
                            gstate = {}
                            v_t, v_b = vst[0]
                            pq_t, pq_b = pqst[0]
                            ga_t, ga_b = gast[0]
                            mx_t, mx_b = mxst[blk % 2]
                            for tt in range(tpb):
                                t = blk * tpb + tt
                                tok = slice(tt * 128, (tt + 1) * 128)

                                if pb >= 0:
                                    g1(pb, tt)

                                def tm(c0, width, bi, Wsrc=WT, Wb=b_WT):
                                    for j in range(8):
                                        fw.op(PE, (lambda j=j: nc.tensor.matmul(
                                            bank(bi, width), lhsT=hT_t[:, j, tok], rhs=Wsrc[:, j, c0:c0 + width],
                                            start=(j == 0), stop=(j == 7))),
                                            reads=[Wb, hT_b], writes=[bankb[bi]], signal=(j == 7))
                                bi = next_bank(); tm(0, 512, bi)
                                fw.op(ACT, (lambda bi=bi: nc.scalar.copy(out=v_t[:, tt, :], in_=bank(bi))),
                                      reads=[bankb[bi]], writes=[v_b])
                                if not J.sample:
                                    kt_, kb_ = kvst[kv_rr[0] % 2]; kv_rr[0] += 1
                                    fw.op(DVE, (lambda bi=bi, kt_=kt_: nc.vector.tensor_copy(out=kt_[:], in_=bank(bi))),
                                          reads=[bankb[bi]], writes=[kb_])
                                    fw.dma(POOL, [(nv_o[J.idx - 1, l][:, t * 128:(t + 1) * 128, :].rearrange("h t e -> t h e"),
                                                   kt_[:].rearrange("p (h e) -> p h e", h=4))], kb_, reads=[kb_])
                                    bi = next_bank(); tm(512, 512, bi, WF, b_WF)
                                    kt_, kb_ = kvst[kv_rr[0] % 2]; kv_rr[0] += 1
                                    fw.op(DVE, (lambda bi=bi, kt_=kt_: nc.vector.tensor_copy(out=kt_[:], in_=bank(bi))),
                                          reads=[bankb[bi]], writes=[kb_])
                                    fw.dma(POOL, [(nk_o[J.idx - 1, l][:, t * 128:(t + 1) * 128, :].rearrange("h t e -> t h e"),
                                                   kt_[:].rearrange("p (h e) -> p h e", h=4))], kb_, reads=[kb_])
                                bi = next_bank(); tm(512, 512, bi)
                                fw.op(ACT, (lambda bi=bi: nc.scalar.copy(out=pq_t[:, tt, :], in_=bank(bi))),
                                      reads=[bankb[bi]], writes=[pq_b])
                                bi = next_bank(); tm(2048, 512, bi)
                                fw.op(ACT, (lambda bi=bi: nc.scalar.activation(out=ga_t[:, tt, :], in_=bank(bi), func=AF.Silu)),
                                      reads=[bankb[bi]], writes=[ga_b])
                                b_uvm = next_bank(); tm(1024, 512, b_uvm)
                                b_gm = next_bank(); tm(1536, 256, b_gm)
                                fw.op(ACT, (lambda b_gm=b_gm: nc.scalar.activation(out=sgm4[:, tt, :], in_=bank(b_gm, 256), func=AF.Silu)),
                                      reads=[bankb[b_gm]], writes=[b_sgm])
                                fw.op(DVE, (lambda b_uvm=b_uvm: nc.vector.tensor_copy(out=uv4[:, tt, :], in_=bank(b_uvm))),
                                      reads=[bankb[b_uvm]], writes=[b_uv])
                                for hh in range(4):
                                    fw.op(DVE, (lambda hh=hh: nc.vector.bn_stats(
                                        out=gst[:, hh, :], in_=uv4[:, tt, 256 + hh * 64:256 + (hh + 1) * 64])),
                                        reads=[b_uv], writes=[b_gst])
                                for hh in range(4):
                                    fw.op(DVE, (lambda hh=hh: nc.vector.bn_aggr(out=gmv[:, tt * 4 + hh, :], in_=gst[:, hh, :])),
                                          reads=[b_gst], writes=[b_gmv])
                                if pb >= 0:
                                    if tt > 0:
                                        g4(pb, tt - 1, gstate)
                                    g2(pb, tt, gstate)
                                    g3(pb, tt, gstate)
                            if pb >= 0:
                                g4(pb, tpb - 1, gstate)
                                gstore(pb)
                            fw.op(DVE, lambda: nc.vector.tensor_scalar(
                                out=gve[:], in0=gmv[:, :, 1], scalar1=LN_EPS, scalar2=None, op0=ALU.add),
                                reads=[b_gmv], writes=[b_gve])
                            fw.op(POOL, lambda: nc.gpsimd.tensor_tensor(out=grs[:], in0=gve[:], in1=nhalf[:, 0:tpb * 4], op=ALU.pow),
                                  reads=[b_gve, b_nh], writes=[b_grs])
                            r0 = J.npast + blk * NQ
                            fw.dma(POOL, [(J.V[r0:r0 + NQ, :].rearrange("(t p) c -> p t c", p=128), v_t[:])], v_b,
                                   reads=[v_b], writes=[J.b_v[blk]])
                            fw.dma(POOL, [(J.PQ[blk * NQ:(blk + 1) * NQ, :].rearrange("(t p) c -> p t c", p=128), pq_t[:])],
                                   pq_b, reads=[pq_b], writes=[J.b_pq[blk]])
                            fw.dma(POOL, [(J.SGA[blk * NQ:(blk + 1) * NQ, :].rearrange("(t p) c -> p t c", p=128), ga_t[:])],
                                   ga_b, reads=[ga_b], writes=[J.b_sga[blk]])

                        def g1(blk, tt):
                            uv4, b_uv = uv4s[blk % 2]
                            gmv, b_gmv = gmvs[blk % 2]
                            grs, b_grs = grss[blk % 2]
                            vnb, b_vnb = vnbs[tt]
                            for hh in range(4):
                                fw.op(DVE, (lambda hh=hh: nc.vector.tensor_scalar(
                                    out=vnb[:, hh * 64:(hh + 1) * 64], in0=uv4[:, tt, 256 + hh * 64:256 + (hh + 1) * 64],
                                    scalar1=gmv[:, tt * 4 + hh, 0:1], scalar2=grs[:, tt * 4 + hh:tt * 4 + hh + 1],
                                    op0=ALU.subtract, op1=ALU.mult)),
                                    reads=[b_uv, b_gmv, b_grs], writes=[b_vnb])

                        def g2(blk, tt, st):
                            vnb, b_vnb = vnbs[tt]
                            b_s = next_bank()
                            st[('s', tt)] = b_s
                            for hh in range(4):
                                fw.op(PE, (lambda hh=hh, b_s=b_s: nc.tensor.matmul(
                                    bank(b_s)[:, hh * 64:(hh + 1) * 64], lhsT=wsT[:, hh, :],
                                    rhs=vnb[:, hh * 64:(hh + 1) * 64], start=True, stop=True, skip_group_check=True)),
                                    reads=[b_wsT, b_vnb], writes=[bankb[b_s]], signal=(hh == 3))

                        def g3(blk, tt, st):
                            uv4, b_uv = uv4s[blk % 2]
                            sgm4, b_sgm = sgm4s[blk % 2]
                            gmo, b_gmo = gmos[tt]
                            b_s = st[('s', tt)]
                            for hh in range(4):
                                fw.op(DVE, (lambda hh=hh, b_s=b_s: nc.vector.scalar_tensor_tensor(
                                    out=gmo[:, hh * 64:(hh + 1) * 64], in0=bank(b_s)[:, hh * 64:(hh + 1) * 64],
                                    scalar=bsT[:, hh:hh + 1], in1=uv4[:, tt, hh * 64:(hh + 1) * 64],
                                    op0=ALU.add, op1=ALU.mult)),
                                    reads=[bankb[b_s], b_bsT, b_uv], writes=[b_gmo])
                            fw.op(DVE, lambda: nc.vector.tensor_tensor(out=gmo[:], in0=gmo[:], in1=sgm4[:, tt, :], op=ALU.mult),
                                  reads=[b_gmo, b_sgm], writes=[b_gmo])

                        def g4(blk, tt, st):
                            mx_t, mx_b = mxst[blk % 2]
                            gmo, b_gmo = gmos[tt]
                            tok = slice(tt * 128, (tt + 1) * 128)
                            b_t = next_bank()
                            for cc in range(2):
                                fw.op(PE, (lambda cc=cc, b_t=b_t: nc.tensor.transpose(
                                    out=bank(b_t)[:, cc * 128:(cc + 1) * 128], in_=gmo[:, cc * 128:(cc + 1) * 128],
                                    identity=identf[:])),
                                    reads=[b_gmo, b_identf], writes=[bankb[b_t]], signal=(cc == 1))
                            fw.op(ACT, (lambda b_t=b_t: nc.scalar.copy(
                                out=mx_t[:, :, tok], in_=bank(b_t, 256).rearrange("p (c t) -> p c t", c=2))),
                                reads=[bankb[b_t]], writes=[mx_b])

                        def gstore(blk):
                            mx_t, mx_b = mxst[blk % 2]
                            fw.dma(POOL, [(J.MIXT[768:1024, blk * NQ:(blk + 1) * NQ].rearrange("(c p) t -> p c t", p=128),
                                           mx_t[:])], mx_b, reads=[mx_b], writes=[J.b_mix[blk]])

                        def gtail(blk):
                            st = {}
                            for tt in range(tpb):
                                g1(blk, tt)
                                g2(blk, tt, st)
                                g3(blk, tt, st)
                                g4(blk, tt, st)
                            gstore(blk)

                        ln_stats(0)
                        ln_apply(0)
                        for blk in range(J.nblk):
                            if blk + 1 < J.nblk:
                                ln_stats(blk + 1)
                            fm(blk)
                            if blk + 1 < J.nblk:
                                ln_apply(blk + 1)
                            tmaj(blk)
                        gtail(J.nblk - 1)
                        fw.barrier()
                        fw.release(phase_bufs)

                chk('p1')
                for J in jobs:
                    with ExitStack() as ph:
                        NQ, tpb, Lk = J.NQ, J.tpb, J.Lk
                        nkc = Lk // 128
                        KT0 = [sb(ph, [128, Lk], BF16, "KT0") for _ in range(2)]
                        KT1 = [sb(ph, [128, Lk], BF16, "KT1") for _ in range(2)]
                        QTs = [sb(ph, [128, J.L], BF16, "QTs") for _ in range(2)]
                        V1 = [sb(ph, [128, nkc, 132], BF16, "V1") for _ in range(2)]
                        NE = 3
                        Et = [sb(ph, [128, 2 * NQ], BF16, "E") for _ in range(NE)]
                        Oc, b_Oc = sb(ph, [128, 3 * 512], F32, "Oc")
                        sga_t = [sb(ph, [128, tpb, 128], BF16, "sga") for _ in range(2)]
                        rr, b_rr = sb(ph, [128, 4], F32, "rr")
                        ta, b_ta = sb(ph, [128, 128], F32, "ta")
                        to4, b_to = sb(ph, [128, tpb, 128], F32, "to")
                        ms4, b_ms = sb(ph, [128, tpb], F32, "ms4")
                        me4, b_me = sb(ph, [128, tpb], F32, "me4")
                        rq4, b_rq = sb(ph, [128, tpb], F32, "rq4")
                        tj, b_tj = sb(ph, [128, 128], F32, "tj")
                        tg = [sb(ph, [128, 128], F32, "tg") for _ in range(tpb)]
                        mst = [sb(ph, [128, NQ], BF16, "mst") for _ in range(2)]
                        phase_bufs = [b_Oc, b_rr, b_ta, b_to, b_tj, b_ms, b_me, b_rq]
                        for lst in (KT0, KT1, QTs, V1, Et, sga_t, tg, mst):
                            phase_bufs += [b for (_, b) in lst]
                        for i in range(2):
                            fw.op(POOL, (lambda i=i: nc.gpsimd.memset(KT0[i][0][64:128, :], 0.0)), writes=[KT0[i][1]])
                            fw.op(POOL, (lambda i=i: nc.gpsimd.memset(KT1[i][0][0:64, :], 0.0)), writes=[KT1[i][1]])
                            fw.op(POOL, (lambda i=i: nc.gpsimd.memset(V1[i][0][:, :, 128:132], 1.0)), writes=[V1[i][1]])
                        qk_deps = list(J.b_qk) + ([J.b_kpast] if J.sample else [])
                        v_deps = list(J.b_v) + ([J.b_vpast] if J.sample else [])
                        nreg = 2 * tpb

                        def oreg(m, sub):
                            idx = m * tpb + sub
                            bk = 4 + idx // 3
                            off = (idx % 3) * 129
                            return bk, off, idx
                        e_rr = 0
                        g_rr = [0]
                        sg_rr = [0]
                        pending = []
                        pending2 = []
                        for hh in range(H):
                            k0_t, k0_b = KT0[hh % 2]
                            k1_t, k1_b = KT1[hh % 2]
                            q_t, q_b = QTs[hh % 2]
                            v1_t, v1_b = V1[hh % 2]
                            fw.dma(SP, [(k0_t[0:64, :], J.KT[hh][0:64, :])], k0_b, reads=qk_deps, writes=[k0_b])
                            fw.dma(SP, [(k1_t[64:128, :], J.KT[hh][64:128, :])], k1_b, reads=qk_deps, writes=[k1_b])
                            fw.dma(SP, [(q_t[:], J.QT[hh])], q_b, reads=qk_deps, writes=[q_b])
                            fw.dma(SP, [(v1_t[:, :, 0:128],
                                         J.V[:, hh * 128:(hh + 1) * 128].rearrange("(k p) e -> p k e", p=128))],
                                   v1_b, reads=v_deps, writes=[v1_b])
                            for qb in range(J.nblk):
                                sg_t, sg_b = sga_t[sg_rr[0] % 2]; sg_rr[0] += 1
                                fw.dma(SP, [(sg_t[:], J.SGA[qb * NQ:(qb + 1) * NQ, hh * 128:(hh + 1) * 128]
                                             .rearrange("(t p) e -> p t e", p=128))], sg_b, reads=[J.b_sga[qb]], writes=[sg_b])
                                qs = slice(qb * NQ, (qb + 1) * NQ)

                                def qk(kc):
                                    sbk = (kc % 2) * 2
                                    ks = slice(kc * 128, (kc + 1) * 128)
                                    fw.op(PE, lambda: nc.tensor.matmul(bank(sbk, NQ), lhsT=k0_t[0:64, ks], rhs=q_t[0:64, qs],
                                                                       start=True, stop=True),
                                          reads=[k0_b, q_b], writes=[bankb[sbk]], signal=False)
                                    fw.op(PE, lambda: nc.tensor.matmul(bank(sbk + 1, NQ), lhsT=k1_t[64:128, ks], rhs=q_t[64:128, qs],
                                                                       start=True, stop=True),
                                          reads=[k1_b, q_b], writes=[bankb[sbk], bankb[sbk + 1]])
                                qk(0)
                                if nkc > 1:
                                    qk(1)
                                for kc in range(nkc):
                                    if kc == min(24, nkc - 1) and pending:
                                        pending.pop(0)()
                                    if kc == min(27, nkc - 1) and pending2:
                                        pending2.pop(0)()
                                    sbk = (kc % 2) * 2
                                    e_t, e_b = Et[e_rr % NE]; e_rr += 1
                                    src = ps[:, sbk * 512:(sbk + 2) * 512].rearrange("p (m c) -> p m c", m=2)[:, :, 0:NQ]
                                    fw.op(ACT, (lambda e_t=e_t, src=src: nc.scalar.activation(
                                        out=e_t[:].rearrange("p (m c) -> p m c", m=2), in_=src, func=AF.Exp, scale=0.125)),
                                        reads=[bankb[sbk], bankb[sbk + 1]], writes=[e_b])
                                    if kc + 2 < nkc:
                                        qk(kc + 2)
                                    started = set()
                                    for m in range(2):
                                        for sub in range(tpb):
                                            bk, off, idx = oreg(m, sub)
                                            first = (kc == 0 and bk not in started)
                                            started.add(bk)
                                            last = (m == 1 and sub == tpb - 1)
                                            fw.op(PE, (lambda m=m, sub=sub, bk=bk, off=off, first=first, e_t=e_t: nc.tensor.matmul(
                                                bank(bk)[:, off:off + 129],
                                                lhsT=e_t[:, m * NQ + sub * 128: m * NQ + (sub + 1) * 128],
                                                rhs=v1_t[:, kc, 0:129], start=first, stop=(kc == nkc - 1),
                                                skip_group_check=True)),
                                                reads=[e_b, v1_b], writes=[bankb[bk]],
                                                signal=last)
                                fw.op(DVE, lambda: nc.vector.tensor_copy(out=Oc[:], in_=ps[:, 4 * 512:7 * 512]),
                                      reads=[bankb[4], bankb[5], bankb[6]], writes=[b_Oc])
                                def post(hh=hh, qb=qb, sg_t=sg_t, sg_b=sg_b, qs=qs):
                                    m_t, m_b = mst[qb % 2]
                                    for sub in range(tpb):
                                        _, _, i0 = oreg(0, sub)
                                        _, _, i1 = oreg(1, sub)
                                        o0 = (i0 // 3) * 512 + (i0 % 3) * 129
                                        o1 = (i1 // 3) * 512 + (i1 % 3) * 129
                                        fw.op(DVE, lambda: nc.vector.reciprocal(out=rr[:, 0:1], in_=Oc[:, o0 + 128:o0 + 129]),
                                              reads=[b_Oc], writes=[b_rr])
                                        fw.op(DVE, lambda: nc.vector.reciprocal(out=rr[:, 1:2], in_=Oc[:, o1 + 128:o1 + 129]),
                                              reads=[b_Oc], writes=[b_rr])
                                        fw.op(DVE, lambda: nc.vector.tensor_tensor(out=rr[:, 1:2], in0=rr[:, 1:2], in1=lamv[:, 5:6],
                                                                                   op=ALU.mult),
                                              reads=[b_rr, b_lamv], writes=[b_rr])
                                        fw.op(DVE, lambda: nc.vector.tensor_scalar(out=ta[:], in0=Oc[:, o1:o1 + 128],
                                                                                   scalar1=rr[:, 1:2], scalar2=None, op0=ALU.mult),
                                              reads=[b_Oc, b_rr], writes=[b_ta])
                                        fw.op(DVE, lambda: nc.vector.scalar_tensor_tensor(
                                            out=to4[:, sub, :], in0=Oc[:, o0:o0 + 128], scalar=rr[:, 0:1], in1=ta[:],
                                            op0=ALU.mult, op1=ALU.add), reads=[b_Oc, b_rr, b_ta], writes=[b_to])
                                        fw.op(DVE, lambda: nc.vector.tensor_tensor(out=tj[:], in0=to4[:, sub, :], in1=to4[:, sub, :],
                                                                                   op=ALU.mult),
                                              reads=[b_to], writes=[b_tj])
                                        fw.op(DVE, lambda: nc.vector.reduce_sum(out=ms4[:, sub:sub + 1], in_=tj[:],
                                                                                axis=mybir.AxisListType.X),
                                              reads=[b_tj], writes=[b_ms])
                                    fw.op(DVE, lambda: nc.vector.tensor_scalar(
                                        out=me4[:], in0=ms4[:], scalar1=1.0 / 128.0, scalar2=RMS_EPS,
                                        op0=ALU.mult, op1=ALU.add), reads=[b_ms], writes=[b_me])
                                    fw.op(POOL, lambda: nc.gpsimd.tensor_tensor(out=rq4[:], in0=me4[:], in1=nhalf[:, 0:tpb], op=ALU.pow),
                                          reads=[b_me, b_nh], writes=[b_rq])
                                    for sub in range(tpb):
                                        fw.op(DVE, lambda: nc.vector.scalar_tensor_tensor(
                                            out=ta[:], in0=to4[:, sub, :], scalar=rq4[:, sub:sub + 1], in1=SW[:],
                                            op0=ALU.mult, op1=ALU.mult),
                                            reads=[b_to, b_rq, b_SW], writes=[b_ta])
                                        g_t, g_b = tg[sub]
                                        fw.op(DVE, (lambda g_t=g_t, sub=sub: nc.vector.tensor_tensor(
                                            out=g_t[:], in0=ta[:], in1=sg_t[:, sub, :], op=ALU.mult)),
                                            reads=[b_ta, sg_b], writes=[g_b])
                                    for sub in range(tpb):
                                        g_t, g_b = tg[sub]
                                        fw.op(PE, (lambda g_t=g_t, sub=sub: nc.tensor.transpose(
                                            out=bank(7)[:, sub * 128:(sub + 1) * 128], in_=g_t[:], identity=identf[:])),
                                            reads=[g_b, b_identf], writes=[bankb[7]], signal=(sub == tpb - 1))
                                    def postB(m_t=m_t, m_b=m_b, hh=hh, qs=qs, qb=qb):
                                        fw.op(ACT, (lambda m_t=m_t: nc.scalar.copy(out=m_t[:, 0:tpb * 128], in_=bank(7, tpb * 128))),
                                              reads=[bankb[7]], writes=[m_b])
                                        fw.dma(POOL, [(J.MIXT[hh * 128:(hh + 1) * 128, qs], m_t[:])], m_b, reads=[m_b],
                                               writes=[J.b_mix[qb]])
                                    pending2.append(postB)
                                pending.append(post)
                        while pending:
                            pending.pop(0)()
                        while pending2:
                            pending2.pop(0)()
                        fw.barrier()
                        fw.release(phase_bufs)

                chk('p2')
                for J in jobs:
                    with ExitStack() as ph:
                        NQ = J.NQ
                        nlc = J.L // 128
                        GL = 4 if J.sample else 2
                        tab_c = dfts_c if J.sample else dftp_c
                        tab_s = dfts_s if J.sample else dftp_s
                        pq_sb, b_pqs = sb(ph, [128, nlc, 512], BF16, "pqsb")
                        tc_ = [sb(ph, [128, GL, NQ], BF16, "tabc") for _ in range(2)]
                        ts_ = [sb(ph, [128, GL, NQ], BF16, "tabs") for _ in range(2)]
                        sgf_t = [sb(ph, [128, NQ], BF16, "sgft") for _ in range(2)]
                        fo = [sb(ph, [128, NQ], BF16, "fo") for _ in range(2)]
                        phase_bufs = [b_pqs] + [b for lst in (tc_, ts_, sgf_t, fo) for (_, b) in lst]
                        fw.dma(SP, [(pq_sb[:], J.PQ.rearrange("(k p) c -> p k c", p=128))], b_pqs, reads=list(J.b_pq),
                               writes=[b_pqs])
                        fscale = 1.0 / math.sqrt(64.0 * J.L)
                        gi = 0
                        oi = 0
                        use_sym = J.sample and J.nblk >= 2
                        if use_sym:
                            L = J.L
                            wsb = [sb(ph, [128, 512], F32, "wsb") for _ in range(2)]
                            at_ = [sb(ph, [128, 512], F32, "fa") for _ in range(2)]
                            bt_ = [sb(ph, [128, 512], F32, "fb") for _ in range(2)]
                            sgh = [sb(ph, [128, 512], BF16, "sgh") for _ in range(2)]
                            fh = [sb(ph, [128, 512], BF16, "fh") for _ in range(2)]
                            fn_, b_fn = sb(ph, [128, 4], BF16, "fn")
                            sgn, b_sgn = sb(ph, [128, 4], BF16, "sgn")
                            phase_bufs += [b_fn, b_sgn] + [b for lst in (wsb, at_, bt_, sgh, fh) for (_, b) in lst]
                            for kb in range(J.nblk // 2):
                                bU = [next_bank(), next_bank()]
                                bW = [next_bank(), next_bank()]
                                for g0 in range(0, nlc, GL):
                                    c_t, c_b = tc_[gi % 2]
                                    s_t, s_b = ts_[gi % 2]
                                    gi += 1
                                    fw.dma(SP, [(c_t[:], tab_c[kb][:, g0:g0 + GL, :])], c_b, writes=[c_b])
                                    fw.dma(SP, [(s_t[:], tab_s[kb][:, g0:g0 + GL, :])], s_b, writes=[s_b])
                                    for gl in range(GL):
                                        lc = g0 + gl
                                        for cc in range(2):
                                            fw.op(PE, lambda: nc.tensor.matmul(
                                                bank(bU[cc]), lhsT=pq_sb[:, lc, cc * 128:(cc + 1) * 128], rhs=c_t[:, gl, :],
                                                start=(lc == 0), stop=(lc == nlc - 1)),
                                                reads=[b_pqs, c_b], writes=[bankb[bU[cc]]], signal=False)
                                            fw.op(PE, lambda: nc.tensor.matmul(
                                                bank(bW[cc]), lhsT=pq_sb[:, lc, 256 + cc * 128:256 + (cc + 1) * 128],
                                                rhs=s_t[:, gl, :], start=(lc == 0), stop=(lc == nlc - 1)),
                                                reads=[b_pqs, s_b], writes=[bankb[bU[cc]], bankb[bW[cc]]],
                                                signal=(gl == GL - 1))
                                k0h = L - kb * 512 - 511
                                nh_ = 511 if kb == 0 else 512
                                hb = sorted(set([k0h // 512, (k0h + nh_ - 1) // 512]))
                                for cc in range(2):
                                    w_t, w_b = wsb[oi % 2]
                                    a_t, a_b = at_[oi % 2]
                                    b_t, b_b = bt_[oi % 2]
                                    sg_t, sg_b = sgf_t[oi % 2]
                                    sh_t, sh_b = sgh[oi % 2]
                                    f_t, f_b = fo[oi % 2]
                                    h_t, h_b = fh[oi % 2]
                                    oi += 1
                                    rows = slice(cc * 128, (cc + 1) * 128)
                                    fw.dma(SP, [(sg_t[:], J.SGF[rows, kb * 512:(kb + 1) * 512])], sg_b,
                                           reads=[J.b_sgf[kb]], writes=[sg_b])
                                    fw.dma(SP, [(sh_t[:, 0:nh_], J.SGF[rows, k0h:k0h + nh_])], sh_b,
                                           reads=[J.b_sgf[i] for i in hb], writes=[sh_b])
                                    fw.op(ACT, lambda: nc.scalar.copy(out=w_t[:], in_=bank(bW[cc])),
                                          reads=[bankb[bW[cc]]], writes=[w_b])
                                    fw.op(DVE, lambda: nc.vector.tensor_tensor(out=a_t[:], in0=bank(bU[cc]), in1=w_t[:], op=ALU.add),
                                          reads=[bankb[bU[cc]], w_b], writes=[a_b])
                                    fw.op(DVE, lambda: nc.vector.tensor_tensor(out=b_t[:], in0=bank(bU[cc]), in1=w_t[:], op=ALU.subtract),
                                          reads=[bankb[bU[cc]], w_b], writes=[b_b])
                                    fw.op(DVE, lambda: nc.vector.scalar_tensor_tensor(
                                        out=f_t[:], in0=a_t[:], scalar=fscale, in1=sg_t[:], op0=ALU.mult, op1=ALU.mult),
                                        reads=[a_b, sg_b], writes=[f_b])
                                    fw.op(DVE, lambda: nc.vector.scalar_tensor_tensor(
                                        out=h_t[:, 0:nh_], in0=b_t[:, ::-1][:, 0:nh_], scalar=fscale, in1=sh_t[:, 0:nh_],
                                        op0=ALU.mult, op1=ALU.mult),
                                        reads=[b_b, sh_b], writes=[h_b])
                                    fw.dma(POOL, [(J.MIXT[512 + cc * 128:512 + (cc + 1) * 128, kb * 512:(kb + 1) * 512], f_t[:])],
                                           f_b, reads=[f_b], writes=[J.b_mix[kb]])
                                    fw.dma(POOL, [(J.MIXT[512 + cc * 128:512 + (cc + 1) * 128, k0h:k0h + nh_], h_t[:, 0:nh_])],
                                           h_b, reads=[h_b], writes=[J.b_mix[i] for i in hb])
                            kN = L // 2
                            fw.dma(SP, [(sgn[:, cc:cc + 1], J.SGF[cc * 128:(cc + 1) * 128, kN:kN + 1]) for cc in range(2)],
                                   b_sgn, reads=[J.b_sgf[kN // 512]], writes=[b_sgn], allow_slow_non_contiguous=True)
                            for cc in range(2):
                                for lc in range(nlc):
                                    fw.op(PE, lambda: nc.tensor.matmul(
                                        bank(cc)[:, 0:2], lhsT=pq_sb[:, lc, cc * 128:(cc + 1) * 128], rhs=altt[:, 0:2],
                                        start=(lc == 0), stop=(lc == nlc - 1)),
                                        reads=[b_pqs, b_alt], writes=[bankb[cc]], signal=(lc == nlc - 1))
                                fw.op(DVE, lambda: nc.vector.scalar_tensor_tensor(
                                    out=fn_[:, cc:cc + 1], in0=bank(cc)[:, 0:1], scalar=fscale, in1=sgn[:, cc:cc + 1],
                                    op0=ALU.mult, op1=ALU.mult),
                                    reads=[bankb[cc], b_sgn], writes=[b_fn])
                            fw.dma(POOL, [(J.MIXT[512 + cc * 128:512 + (cc + 1) * 128, kN:kN + 1], fn_[:, cc:cc + 1]) for cc in range(2)],
                                   b_fn, reads=[b_fn], writes=[J.b_mix[kN // 512]], allow_slow_non_contiguous=True)
                        for kb in range(0 if use_sym else J.nblk):
                            bks = [next_bank(), next_bank()]
                            for g0 in range(0, nlc, GL):
                                c_t, c_b = tc_[gi % 2]
                                s_t, s_b = ts_[gi % 2]
                                gi += 1
                                fw.dma(SP, [(c_t[:], tab_c[kb][:, g0:g0 + GL, :])], c_b, writes=[c_b])
                                fw.dma(SP, [(s_t[:], tab_s[kb][:, g0:g0 + GL, :])], s_b, writes=[s_b])
                                for gl in range(GL):
                                    lc = g0 + gl
                                    for cc in range(2):
                                        fw.op(PE, (lambda cc=cc, lc=lc, gl=gl, c_t=c_t: nc.tensor.matmul(
                                            bank(bks[cc], NQ), lhsT=pq_sb[:, lc, cc * 128:(cc + 1) * 128], rhs=c_t[:, gl, :],
                                            start=(lc == 0), stop=False)),
                                            reads=[b_pqs, c_b], writes=[bankb[bks[cc]]], signal=False)
                                        lastmm = (lc == nlc - 1)
                                        fw.op(PE, (lambda cc=cc, lc=lc, gl=gl, s_t=s_t, lastmm=lastmm: nc.tensor.matmul(
                                            bank(bks[cc], NQ), lhsT=pq_sb[:, lc, 256 + cc * 128:256 + (cc + 1) * 128],
                                            rhs=s_t[:, gl, :], start=False, stop=lastmm)),
                                            reads=[b_pqs, s_b], writes=[bankb[bks[cc]]],
                                            signal=(gl == GL - 1))
                            for cc in range(2):
                                sg_t, sg_b = sgf_t[oi % 2]
                                f_t, f_b = fo[oi % 2]
                                oi += 1
                                fw.dma(SP, [(sg_t[:], J.SGF[cc * 128:(cc + 1) * 128, kb * NQ:(kb + 1) * NQ])], sg_b,
                                       reads=[J.b_sgf[kb]], writes=[sg_b])
                                fw.op(DVE, (lambda cc=cc, f_t=f_t, sg_t=sg_t: nc.vector.scalar_tensor_tensor(
                                    out=f_t[:], in0=bank(bks[cc], NQ), scalar=fscale, in1=sg_t[:],
                                    op0=ALU.mult, op1=ALU.mult)),
                                    reads=[bankb[bks[cc]], sg_b], writes=[f_b])
                                fw.dma(POOL, [(J.MIXT[512 + cc * 128:512 + (cc + 1) * 128, kb * NQ:(kb + 1) * NQ], f_t[:])],
                                       f_b, reads=[f_b], writes=[J.b_mix[kb]])
                        fw.barrier()
                        fw.release(phase_bufs)

                chk('p3')
                for J in jobs:
                    with ExitStack() as ph:
                        NQ, tpb = J.NQ, J.tpb
                        G_t, G_b = Gt[J.modrow]
                        mx = [sb(ph, [128, 8, NQ], BF16, "mx") for _ in range(2)]
                        xr = [sb(ph, [128, 1024], F32, "xr") for _ in range(2)]
                        ty = [sb(ph, [128, 1024], F32, "ty") for _ in range(2 * tpb)]
                        yo = [sb(ph, [128, 1024], F32, "yo") for _ in range(3)]
                        st6, b_st6 = sb(ph, [128, 2, 6], F32, "st6")
                        mv4s = [sb(ph, [128, tpb, 2], F32, "mv4") for _ in range(2)]
                        ve4s = [sb(ph, [128, tpb], F32, "ve4") for _ in range(2)]
                        rs4s = [sb(ph, [128, tpb], F32, "rs4") for _ in range(2)]
                        phase_bufs = [b_st6] + [b for lst in (mx, xr, ty, yo, mv4s, ve4s, rs4s) for (_, b) in lst]

                        WOg, b_WOg = sb(ph, [128, 8, 1024], BF16, "WOg")
                        nb4s = [sb(ph, [128, tpb], F32, "nb4") for _ in range(2)]
                        phase_bufs += [b_WOg] + [b for (_, b) in nb4s]
                        for j in range(8):
                            fw.op(DVE, (lambda j=j: nc.vector.tensor_tensor(out=WOg[:, j, :], in0=WO[:, j, :], in1=G_t[:], op=ALU.mult)),
                                  reads=[b_WO, G_b], writes=[b_WOg])

                        junk5, b_junk5 = sb(ph, [128, 1024], F32, "junk5")
                        s12s = [sb(ph, [128, 2, tpb], F32, "s12") for _ in range(2)]
                        phase_bufs += [b_junk5] + [b for (_, b) in s12s]

                        def stageA(blk):
                            mx_t, mx_b = mx[blk % 2]
                            mv4, b_mv4 = mv4s[blk % 2]
                            s12, b_s12 = s12s[blk % 2]
                            fw.dma(SP, [(mx_t[:], J.MIXT[:, blk * NQ:(blk + 1) * NQ].rearrange("(c p) t -> p c t", p=128))],
                                   mx_b, reads=[J.b_mix[blk]], writes=[mx_b])
                            for tt in range(tpb):
                                t = blk * tpb + tt
                                tok = slice(tt * 128, (tt + 1) * 128)
                                x_t, x_b = xr[t % 2]
                                y_t, y_b = ty[(blk % 2) * tpb + tt]
                                src, sdeps = x_src(J, t)
                                fw.dma(SP, [(x_t[:], src)], x_b, reads=sdeps, writes=[x_b])
                                bks = [next_bank(), next_bank()]
                                for nb in range(2):
                                    for j in range(8):
                                        fw.op(PE, (lambda j=j, nb=nb: nc.tensor.matmul(
                                            bank(bks[nb]), lhsT=mx_t[:, j, tok], rhs=WOg[:, j, nb * 512:(nb + 1) * 512],
                                            start=(j == 0), stop=(j == 7))),
                                            reads=[mx_b, b_WOg], writes=[bankb[bks[nb]]], signal=(j == 7))
                                for nb in range(2):
                                    cs_ = slice(nb * 512, (nb + 1) * 512)
                                    fw.op(DVE, (lambda nb=nb, cs_=cs_, y_t=y_t, x_t=x_t: nc.vector.scalar_tensor_tensor(
                                        out=y_t[:, cs_], in0=x_t[:, cs_], scalar=ALPHA, in1=bank(bks[nb]),
                                        op0=ALU.mult, op1=ALU.add)),
                                        reads=[bankb[bks[nb]], x_b], writes=[y_b])
                                fw.op(ACT, (lambda y_t=y_t: nc.scalar.activation(
                                    out=junk5[:], in_=y_t[:], func=AF.Identity, accum_out=s12[:, 0, tt:tt + 1])),
                                    reads=[y_b], writes=[b_junk5, b_s12])
                                fw.op(ACT, (lambda y_t=y_t: nc.scalar.activation(
                                    out=junk5[:], in_=y_t[:], func=AF.Square, accum_out=s12[:, 1, tt:tt + 1])),
                                    reads=[y_b], writes=[b_junk5, b_s12])
                            ve4, b_ve4 = ve4s[blk % 2]
                            rs4, b_rs4 = rs4s[blk % 2]
                            fw.op(DVE, lambda: nc.vector.tensor_scalar(
                                out=mv4[:, :, 0], in0=s12[:, 0, :], scalar1=1.0 / D, scalar2=None, op0=ALU.mult),
                                reads=[b_s12], writes=[b_mv4])
                            fw.op(DVE, lambda: nc.vector.tensor_tensor(out=mv4[:, :, 1], in0=mv4[:, :, 0], in1=mv4[:, :, 0], op=ALU.mult),
                                  reads=[b_mv4], writes=[b_mv4])
                            fw.op(DVE, lambda: nc.vector.scalar_tensor_tensor(
                                out=mv4[:, :, 1], in0=s12[:, 1, :], scalar=1.0 / D, in1=mv4[:, :, 1], op0=ALU.mult, op1=ALU.subtract),
                                reads=[b_s12, b_mv4], writes=[b_mv4])
                            fw.op(DVE, lambda: nc.vector.tensor_scalar(
                                out=ve4[:], in0=mv4[:, :, 1], scalar1=LN_EPS, scalar2=None, op0=ALU.add),
                                reads=[b_mv4], writes=[b_ve4])
                            fw.op(POOL, lambda: nc.gpsimd.tensor_tensor(out=rs4[:], in0=ve4[:], in1=nhalf[:, 0:tpb], op=ALU.pow),
                                  reads=[b_ve4, b_nh], writes=[b_rs4])
                            nb4, b_nb4 = nb4s[blk % 2]
                            fw.op(DVE, lambda: nc.vector.scalar_tensor_tensor(
                                out=nb4[:], in0=mv4[:, :, 0], scalar=-1.0, in1=rs4[:], op0=ALU.mult, op1=ALU.mult),
                                reads=[b_mv4, b_rs4], writes=[b_nb4])

                        def stageB(blk):
                            mv4, b_mv4 = mv4s[blk % 2]
                            rs4, b_rs4 = rs4s[blk % 2]
                            nb4, b_nb4 = nb4s[blk % 2]
                            for tt in range(tpb):
                                t = blk * tpb + tt
                                y_t, y_b = ty[(blk % 2) * tpb + tt]
                                o_t, o_b = yo[t % 3]
                                fw.op(ACT, (lambda y_t=y_t, o_t=o_t: nc.scalar.activation(
                                    out=o_t[:], in_=y_t[:], func=AF.Identity, bias=nb4[:, tt:tt + 1], scale=rs4[:, tt:tt + 1])),
                                    reads=[y_b, b_nb4, b_rs4], writes=[o_b])
                                fw.op(DVE, (lambda o_t=o_t: nc.vector.tensor_tensor(out=o_t[:], in0=o_t[:], in1=lnG[:], op=ALU.mult)),
                                      reads=[o_b, b_lnG], writes=[o_b])
                                fw.op(POOL, (lambda o_t=o_t: nc.gpsimd.tensor_tensor(out=o_t[:], in0=o_t[:], in1=lnB[:], op=ALU.add)),
                                      reads=[o_b, b_lnB], writes=[o_b])
                                dst, ddeps = x_dst(J, t)
                                fw.dma(POOL, [(dst, o_t[:])], o_b, reads=[o_b], writes=ddeps)

                        stageA(0)
                        for blk in range(J.nblk):
                            if blk + 1 < J.nblk:
                                stageA(blk + 1)
                            stageB(blk)
                        fw.barrier()
                        fw.release(phase_bufs)

        except StopBuild:
            for e_ in fw.engs:
                e_.pending = False

        fw.barrier()
    return nc


_CACHE = {}


def _get_program():
    if "nc" not in _CACHE:
        _CACHE["nc"] = build_program()
        _CACHE["consts"] = _consts(4096, 256)
    return _CACHE["nc"], _CACHE["consts"]


def kernel(x_prompt, x_sample, c, cache_k, cache_v, c_ctx, w_ada, b_ada, w_in, w_out,
           lam_q1, lam_k1, lam_q2, lam_k2, subln_w, fourier_w, gmlp_ws, gmlp_bs, ln_g, ln_b):
    nc, consts = _get_program()
    f = lambda a: np.ascontiguousarray(np.asarray(a, dtype=np.float32))
    x_prompt, x_sample, c, cache_k, cache_v, c_ctx = map(f, (x_prompt, x_sample, c, cache_k, cache_v, c_ctx))
    lam4 = np.concatenate([f(lam_q1), f(lam_k1), f(lam_q2), f(lam_k2)], axis=1)
    shared = {
        "w_ada": f(w_ada), "b_ada": f(b_ada), "w_in": f(w_in), "w_out": f(w_out), "lam4": lam4,
        "subln_w": f(subln_w), "fourier_w": f(fourier_w), "gmlp_ws": f(gmlp_ws), "gmlp_bs": f(gmlp_bs),
        "ln_g": f(ln_g), "ln_b": f(ln_b),
    }
    shared.update(consts)
    in_maps = []
    for i in range(N_CORES):
        m = dict(shared)
        m["xs"] = x_sample[i]
        m["xp"] = x_prompt[2 * i:2 * i + 2]
        m["c2"] = np.stack([c[i], c_ctx], axis=0)
        m["ck"] = cache_k[i]
        m["cv"] = cache_v[i]
        in_maps.append(m)
    res = run_bass_kernel_spmd(nc, in_maps, core_ids=list(range(N_CORES)))
    r = res.results
    y_p = np.concatenate([r[i]["y_p"] for i in range(N_CORES)], axis=0).astype(np.float32)
    y_s = np.stack([r[i]["y_s"] for i in range(N_CORES)], axis=0).astype(np.float32)
    nk = np.concatenate([r[i]["nk"] for i in range(N_CORES)], axis=0).astype(np.float32)
    nv = np.concatenate([r[i]["nv"] for i in range(N_CORES)], axis=0).astype(np.float32)
    return (y_p, y_s, nk, nv)
```

```python
import math
from contextlib import ExitStack

import numpy as np
import ml_dtypes
import concourse.bass as bass
import concourse.mybir as mybir
from concourse.bass_utils import run_bass_kernel_spmd

F32 = mybir.dt.float32
BF16 = mybir.dt.bfloat16
ALU = mybir.AluOpType
AF = mybir.ActivationFunctionType

D = 1024
H = 4
NPAST = 256
LN_EPS = 1e-5
RMS_EPS = 1e-5
DEPTH = 4
ALPHA = (2.0 * DEPTH) ** 0.25
N_CORES = 8


class Sem:
    def __init__(self, h, name):
        self.h = h
        self.name = name
        self.count = 0


class Buf:
    def __init__(self, name):
        self.name = name
        self.w = {}
        self.r = {}
        self.sem = None


class Eng:
    LIMIT = 24000

    def __init__(self, fw, name, h, skip_self=False):
        self.fw = fw
        self.name = name
        self.h = h
        self.skip_self = skip_self
        self.own = set()
        self.known = {}
        self.pending = False
        self.sem = None
        self._new_sem()

    def _new_sem(self):
        self.sem = self.fw.new_sem("e_" + self.name)
        self.own.add(self.sem)

    def wait(self, sem, val):
        if val <= 0:
            return
        if self.skip_self and sem in self.own:
            return
        if self.known.get(sem, 0) >= val:
            return
        assert sem.count >= val, f"wait on unsignalled token {sem.name} {val}>{sem.count} from {self.name}"
        self.h.wait_ge(sem.h, val)
        self.known[sem] = val

    def signal(self, inst, signal=True):
        if signal:
            inst.then_inc(self.sem.h, 1)
            self.sem.count += 1
            tok = (self.sem, self.sem.count)
            self.pending = False
            if self.sem.count >= self.LIMIT:
                self._new_sem()
            return tok
        self.pending = True
        return (self.sem, self.sem.count + 1)


class FW:
    def __init__(self, nc, stack):
        self.nc = nc
        self.stack = stack
        self.nsem = 0
        self.pool = []
        self.uid = 0
        self.pe = Eng(self, "pe", nc.tensor, skip_self=True)
        self.act = Eng(self, "act", nc.scalar)
        self.dve = Eng(self, "dve", nc.vector)
        self.pool_e = Eng(self, "pool", nc.gpsimd)
        self.sp = Eng(self, "sp", nc.sync)
        self.engs = [self.pe, self.act, self.dve, self.pool_e, self.sp]
        self.dma_sems = []

    def new_sem(self, name):
        self.nsem += 1
        h = self.stack.enter_context(self.nc.semaphore(f"{name}_{self.nsem}"))
        return Sem(h, name)

    def dma_sem(self):
        if self.pool:
            return self.pool.pop()
        s = self.new_sem("d")
        self.dma_sems.append(s)
        return s

    def release(self, bufs):
        for b in bufs:
            if b.sem is not None:
                self.pool.append(b.sem)
                b.sem = None

    def name(self, p):
        self.uid += 1
        return f"{p}_{self.uid}"

    def op(self, eng, fn, reads=(), writes=(), signal=True):
        for b in reads:
            for s, v in b.w.items():
                eng.wait(s, v)
        for b in writes:
            for s, v in b.w.items():
                eng.wait(s, v)
            for s, v in b.r.items():
                eng.wait(s, v)
        inst = fn()
        s, v = eng.signal(inst, signal)
        for b in reads:
            if b.r.get(s, 0) < v:
                b.r[s] = v
        for b in writes:
            b.w = {s: v}
            b.r = {}
        return inst

    def dma(self, q, pairs, sb, reads=(), writes=(), **kw):
        if sb.sem is None:
            sb.sem = self.dma_sem()
        sem = sb.sem
        q.wait(sem, sem.count)
        for b in reads:
            for s, v in b.w.items():
                q.wait(s, v)
        for b in writes:
            for s, v in b.w.items():
                q.wait(s, v)
            for s, v in b.r.items():
                q.wait(s, v)
        for (o, i) in pairs:
            q.h.dma_start(out=o, in_=i, **kw).then_inc(sem.h, 16)
            sem.count += 16
        v = sem.count
        for b in reads:
            if b.r.get(sem, 0) < v:
                b.r[sem] = v
        for b in writes:
            b.w = {sem: v}
            b.r = {}

    def barrier(self):
        sems = []
        for e in self.engs:
            assert not e.pending, f"pending unsignalled instruction on {e.name}"
            for s in e.own:
                sems.append(s)
        sems += self.dma_sems
        for e in self.engs:
            for s in sems:
                e.wait(s, s.count)


def _bf16(a):
    return np.asarray(a, dtype=np.float32).astype(ml_dtypes.bfloat16)


def _consts(LS, LP):
    c = {}
    c["ident_f"] = np.eye(128, dtype=np.float32)
    pm = np.zeros((128, 128), np.float32)
    for p in range(128):
        d = p % 32
        partner = p + 16 if d < 16 else p - 16
        pm[partner, p] = 1.0
    c["pm_b"] = _bf16(pm)
    t = np.arange(LS)
    row = (t // 64).astype(np.float64)
    col = (t % 64).astype(np.float64)
    inv = 10000.0 ** (-np.arange(16, dtype=np.float64) / 16.0)
    cos = np.zeros((128, LS), np.float64)
    sin = np.zeros((128, LS), np.float64)
    for p in range(128):
        d = p % 64
        i = d % 16
        pos = row if d < 32 else col
        ang = pos.astype(np.float32).astype(np.float64) * np.float32(inv[i]).astype(np.float64)
        cos[p] = np.cos(ang)
        sgn = -1.0 if (d % 32) < 16 else 1.0
        sin[p] = sgn * np.sin(ang)
    c["rope_cos"] = cos.astype(np.float32)
    c["rope_sin"] = sin.astype(np.float32)

    def dft(L, nb):
        l = np.arange(L, dtype=np.int64)
        kl = (l[:, None] * l[None, :]) % L
        ang = 2.0 * np.pi * kl.astype(np.float64) / L
        C = np.cos(ang)
        S = -np.sin(ang)
        nlc = L // 128
        kb = L // nb
        def lay(M):
            M4 = M.reshape(nlc, 128, nb, kb).transpose(2, 1, 0, 3)
            return _bf16(np.ascontiguousarray(M4))
        return lay(C), lay(S)
    fc_, fs_ = dft(LS, LS // 512)
    nh = max(1, (LS // 512) // 2)
    c["dfts_c"], c["dfts_s"] = np.ascontiguousarray(fc_[:nh]), np.ascontiguousarray(fs_[:nh])
    alt = np.ones((128, 2), np.float32)
    alt[1::2, :] = -1.0
    c["alt_b"] = _bf16(alt)
    c["dftp_c"], c["dftp_s"] = dft(LP, 1)
    a = 2.0 * np.pi * np.outer(np.arange(64), np.arange(64)) / 64.0
    z = np.zeros((128, 128), np.float32)
    cb = z.copy(); sbd = z.copy()
    for hf in range(2):
        cb[hf * 64:(hf + 1) * 64, hf * 64:(hf + 1) * 64] = np.cos(a)
        sbd[hf * 64:(hf + 1) * 64, hf * 64:(hf + 1) * 64] = np.sin(a)
    c["c64"] = cb
    c["s64"] = sbd
    return c


class StopBuild(Exception):
    pass


def build_program(depth=DEPTH, LS=4096, LP=256, NPR=2, stop=None):
    nc = bass.Bass("TRN2", target_bir_lowering=False)

    def din(name, shape, dt=F32):
        return nc.dram_tensor(name, list(shape), dt, kind="ExternalInput").ap()

    def dout(name, shape, dt=F32):
        return nc.dram_tensor(name, list(shape), dt, kind="ExternalOutput").ap()

    def dscr(name, shape, dt):
        return nc.dram_tensor(name, list(shape), dt, kind="Internal").ap()

    xs_in = din("xs", [LS, D])
    xp_in = din("xp", [NPR, LP, D])
    c2 = din("c2", [2, D])
    ck = din("ck", [DEPTH, H, NPAST, 128])
    cv = din("cv", [DEPTH, H, NPAST, 128])
    w_ada = din("w_ada", [DEPTH, D, 3 * D])
    b_ada = din("b_ada", [DEPTH, 3 * D])
    w_in = din("w_in", [DEPTH, D, 3328])
    w_out = din("w_out", [DEPTH, D, D])
    lam4 = din("lam4", [DEPTH, 256])
    subln = din("subln_w", [DEPTH, 128])
    four_w = din("fourier_w", [DEPTH, 4, 64, 64])
    g_ws = din("gmlp_ws", [DEPTH, 4, 128, 128])
    g_bs = din("gmlp_bs", [DEPTH, 4, 128])
    ln_g = din("ln_g", [DEPTH, D])
    ln_b = din("ln_b", [DEPTH, D])
    ident_f = din("ident_f", [128, 128])
    pm_b = din("pm_b", [128, 128], BF16)
    rope_cos = din("rope_cos", [128, LS])
    rope_sin = din("rope_sin", [128, LS])
    NKB_S = LS // 512
    NKH = max(1, NKB_S // 2)
    dfts_c = din("dfts_c", [NKH, 128, LS // 128, 512], BF16)
    dfts_s = din("dfts_s", [NKH, 128, LS // 128, 512], BF16)
    alt_b = din("alt_b", [128, 2], BF16)
    dftp_c = din("dftp_c", [1, 128, LP // 128, LP], BF16)
    dftp_s = din("dftp_s", [1, 128, LP // 128, LP], BF16)
    c64 = din("c64", [128, 128])
    s64 = din("s64", [128, 128])

    y_s = dout("y_s", [LS, D])
    y_p = dout("y_p", [NPR, LP, D])
    nk_o = dout("nk", [NPR, DEPTH, H, LP, 128])
    nv_o = dout("nv", [NPR, DEPTH, H, LP, 128])

    class Job:
        pass
    jobs = []
    for ji in range(1 + NPR):
        J = Job()
        J.idx = ji
        J.sample = (ji == 0)
        J.L = LS if J.sample else LP
        J.npast = NPAST if J.sample else 0
        J.Lk = J.L + J.npast
        J.modrow = 0 if J.sample else 1
        J.x_in = xs_in if J.sample else xp_in[ji - 1]
        J.y_out = y_s if J.sample else y_p[ji - 1]
        J.xscr = [dscr(f"xscr{ji}_{i}", [J.L, D], F32) for i in range(2)]
        J.QT = dscr(f"qt{ji}", [H, 128, J.L], BF16)
        J.KT = dscr(f"kt{ji}", [H, 128, J.Lk], BF16)
        J.V = dscr(f"v{ji}", [J.Lk, 512], BF16)
        J.PQ = dscr(f"pq{ji}", [J.L, 512], BF16)
        J.SGA = dscr(f"sga{ji}", [J.L, 512], BF16)
        J.SGF = dscr(f"sgf{ji}", [256, J.L], BF16)
        J.MIXT = dscr(f"mixt{ji}", [D, J.L], BF16)
        J.NQ = 512 if J.sample else 256
        J.nblk = J.L // J.NQ
        J.tpb = J.NQ // 128
        J.b_x = {}
        J.b_qk = [Buf(f"qk{ji}_{b}") for b in range(J.nblk)]
        J.b_kpast = Buf(f"kpast{ji}")
        J.b_v = [Buf(f"v{ji}_{b}") for b in range(J.nblk)]
        J.b_vpast = Buf(f"vpast{ji}")
        J.b_pq = [Buf(f"pq{ji}_{b}") for b in range(J.nblk)]
        J.b_sga = [Buf(f"sga{ji}_{b}") for b in range(J.nblk)]
        J.b_sgf = [Buf(f"sgf{ji}_{b}") for b in range(J.nblk)]
        J.b_mix = [Buf(f"mix{ji}_{b}") for b in range(J.nblk)]
        J.b_xs = [[Buf(f"xs{ji}_{i}_{t}") for t in range(J.L // 128)] for i in range(2)]
        jobs.append(J)

    mod_d = dscr("mod_d", [DEPTH, 2, 3 * D], F32)
    b_mod = Buf("mod_d")

    with ExitStack() as gs:
        fw = FW(nc, gs)
        PE, ACT, DVE, POOL, SP = fw.pe, fw.act, fw.dve, fw.pool_e, fw.sp
        block = gs.enter_context(nc.Block())

        def sb(stack, shape, dt, name="t"):
            t = stack.enter_context(nc.sbuf_tensor(fw.name(name), list(shape), dt))
            return t, Buf(name)

        ps = gs.enter_context(nc.psum_tensor(fw.name("ps"), [128, 4096], F32))
        bankb = [Buf(f"bank{i}") for i in range(8)]

        def bank(i, w=512):
            return ps[:, i * 512:i * 512 + w]

        identf, b_identf = sb(gs, [128, 128], F32, "identf")
        pmb, b_pmb = sb(gs, [128, 128], BF16, "pmb")
        altt, b_alt = sb(gs, [128, 2], BF16, "alt")
        WF, b_WF = sb(gs, [128, 8, 1280], BF16, "WF")
        WT, b_WT = sb(gs, [128, 8, 2560], BF16, "WT")
        WO, b_WO = sb(gs, [128, 8, 1024], BF16, "WO")
        Gt = [sb(gs, [128, 1024], F32, "G") for _ in range(2)]
        lnG, b_lnG = sb(gs, [128, 1024], F32, "lnG")
        lnB, b_lnB = sb(gs, [128, 1024], F32, "lnB")
        ssm = [sb(gs, [128, 2, 8], F32, "ssm") for _ in range(2)]
        lamt, b_lamt = sb(gs, [128, 256], F32, "lamt")
        lamv, b_lamv = sb(gs, [128, 8], F32, "lamv")
        SW, b_SW = sb(gs, [128, 128], F32, "SW")
        wsT, b_wsT = sb(gs, [128, 4, 128], BF16, "wsT")
        bsT, b_bsT = sb(gs, [128, 4], F32, "bsT")
        AB, b_AB = sb(gs, [128, 2, 256], BF16, "AB")
        nhalf, b_nh = sb(gs, [128, 16], F32, "nhalf")

        fw.dma(SP, [(identf[:], ident_f)], b_identf, writes=[b_identf])
        fw.dma(SP, [(pmb[:], pm_b)], b_pmb, writes=[b_pmb])
        fw.dma(SP, [(altt[:], alt_b)], b_alt, writes=[b_alt])
        fw.op(DVE, lambda: nc.vector.memset(nhalf[:], -0.5), writes=[b_nh])

        def chk(name):
            if stop == name:
                raise StopBuild()

        try:
            with ExitStack() as ph:
                cT, b_cT = sb(ph, [128, 8, 2], F32, "cT")
                cTs, b_cTs = sb(ph, [128, 8, 2], F32, "cTs")
                wa = [sb(ph, [128, 8, 512], F32, "wa") for _ in range(2)]
                bada, b_bada = sb(ph, [2, 3 * D], F32, "bada")
                modr, b_modr = sb(ph, [2, 3 * D], F32, "modr")
                fw.dma(SP, [(cT[:, :, r], c2[r].rearrange("(j p) -> p j", p=128)) for r in range(2)], b_cT,
                       writes=[b_cT], allow_slow_non_contiguous=True)
                fw.op(ACT, lambda: nc.scalar.activation(out=cTs[:], in_=cT[:], func=AF.Silu),
                      reads=[b_cT], writes=[b_cTs])
                it = 0
                for l in range(depth):
                    fw.dma(SP, [(bada[0:1, :], b_ada[l:l + 1, :]), (bada[1:2, :], b_ada[l:l + 1, :])],
                           b_bada, writes=[b_bada])
                    for cb in range(6):
                        wt_, wb_ = wa[it % 2]
                        src = w_ada[l][:, cb * 512:(cb + 1) * 512].rearrange("(j p) n -> p j n", p=128)
                        fw.dma(SP, [(wt_[:, 0:4, :], src[:, 0:4, :]), (wt_[:, 4:8, :], src[:, 4:8, :])],
                               wb_, writes=[wb_])
                        bi = it % 2
                        for j in range(8):
                            fw.op(PE, (lambda j=j, wt_=wt_, bi=bi: nc.tensor.matmul(
                                bank(bi)[0:2, :], lhsT=cTs[:, j, :], rhs=wt_[:, j, :],
                                start=(j == 0), stop=(j == 7))),
                                reads=[b_cTs, wb_], writes=[bankb[bi]], signal=(j == 7))
                        fw.op(DVE, (lambda cb=cb, bi=bi: nc.vector.tensor_tensor(
                            out=modr[:, cb * 512:(cb + 1) * 512], in0=bank(bi)[0:2, :],
                            in1=bada[:, cb * 512:(cb + 1) * 512], op=ALU.add)),
                            reads=[bankb[bi], b_bada], writes=[b_modr])
                        it += 1
                    fw.dma(SP, [(mod_d[l], modr[:])], b_modr, reads=[b_modr], writes=[b_mod])
                fw.barrier()
                fw.release([b_cT, b_cTs, wa[0][1], wa[1][1], b_bada, b_modr])
            chk('prologue')

            cast_rr = [0]

            def load_cast(dst_ap_fn, src_ap, dst_buf, ncols):
                st, sbf = wst[cast_rr[0] % len(wst)]
                cast_rr[0] += 1
                s3 = src_ap.rearrange("(j p) n -> p j n", p=128)
                fw.dma(SP, [(st[:, 0:4, :ncols], s3[:, 0:4, :]), (st[:, 4:8, :ncols], s3[:, 4:8, :])],
                       sbf, writes=[sbf])
                if cast_rr[0] % 2 == 0:
                    fw.op(POOL, lambda: nc.gpsimd.tensor_copy(out=dst_ap_fn(), in_=st[:, :, :ncols]),
                          reads=[sbf], writes=[dst_buf])
                else:
                    fw.op(ACT, lambda: nc.scalar.copy(out=dst_ap_fn(), in_=st[:, :, :ncols]),
                          reads=[sbf], writes=[dst_buf])

            bank_rr = [2]

            def next_bank():
                b = bank_rr[0]
                bank_rr[0] = 2 + (bank_rr[0] - 2 + 1) % 6
                return b

            for l in range(depth):
                lam_init = 0.8 - 0.6 * math.exp(-0.3 * l)

                wi = w_in[l]
                wph = ExitStack()
                wst = [sb(wph, [128, 8, 256], F32, "wst") for _ in range(6)]
                for c0 in range(0, 1024, 256):
                    load_cast(lambda c0=c0: WF[:, :, c0:c0 + 256], wi[:, c0:c0 + 256], b_WF, 256)
                load_cast(lambda: WF[:, :, 1024:1280], wi[:, 2304:2560], b_WF, 256)
                for i, c0 in enumerate(range(1024, 1536, 256)):
                    load_cast(lambda i=i: WT[:, :, i * 256:(i + 1) * 256], wi[:, c0:c0 + 256], b_WT, 256)
                for i, c0 in enumerate(range(2560, 3072, 256)):
                    load_cast(lambda i=i: WT[:, :, 1024 + i * 256:1024 + (i + 1) * 256], wi[:, c0:c0 + 256], b_WT, 256)
                load_cast(lambda: WT[:, :, 1536:1792], wi[:, 3072:3328], b_WT, 256)
                for i, c0 in enumerate(range(1536, 2048, 256)):
                    load_cast(lambda i=i: WT[:, :, 2048 + i * 256:2048 + (i + 1) * 256], wi[:, c0:c0 + 256], b_WT, 256)
                load_cast(lambda: WT[:, :, 1792:2048], wi[:, 2048:2304], b_WT, 256)
                for c0 in range(0, 1024, 256):
                    load_cast(lambda c0=c0: WO[:, :, c0:c0 + 256], w_out[l][:, c0:c0 + 256], b_WO, 256)
                fw.barrier()
                fw.release([b_ for (_, b_) in wst])
                wph.close()
                chk('weights')

                with ExitStack() as ph:
                    for r in range(2):
                        t_, b_ = Gt[r]
                        fw.dma(SP, [(t_[:], mod_d[l, r, 2 * D:3 * D].partition_broadcast(128))], b_, reads=[b_mod], writes=[b_])
                        s_, sb_ = ssm[r]
                        fw.dma(SP, [(s_[:], mod_d[l, r, 0:2 * D].rearrange("(s j p) -> p s j", p=128, j=8))],
                               sb_, reads=[b_mod], writes=[sb_], allow_slow_non_contiguous=True)
                        fw.op(DVE, (lambda s_=s_: nc.vector.tensor_scalar(
                            out=s_[:, 1, :], in0=s_[:, 1, :], scalar1=1.0, scalar2=None, op0=ALU.add)),
                            reads=[sb_], writes=[sb_])
                    fw.dma(SP, [(lnG[:], ln_g[l].partition_broadcast(128))], b_lnG, writes=[b_lnG])
                    fw.dma(SP, [(lnB[:], ln_b[l].partition_broadcast(128))], b_lnB, writes=[b_lnB])
                    fw.dma(SP, [(SW[:], subln[l].partition_broadcast(128))], b_SW, writes=[b_SW])
                    fw.op(DVE, lambda: nc.vector.tensor_scalar(out=SW[:], in0=SW[:], scalar1=(1.0 - lam_init), scalar2=None,
                                                               op0=ALU.mult), reads=[b_SW], writes=[b_SW])
                    fw.dma(SP, [(lamt[:], lam4[l].partition_broadcast(128))], b_lamt, writes=[b_lamt])
                    junk, b_junk = sb(ph, [128, 64], F32, "junk")
                    for i in range(2):
                        fw.op(DVE, (lambda i=i: nc.vector.tensor_tensor(
                            out=junk[:], in0=lamt[:, i * 128:i * 128 + 64],
                            in1=lamt[:, i * 128 + 64:i * 128 + 128], op=ALU.mult)),
                            reads=[b_lamt], writes=[b_junk])
                        fw.op(DVE, (lambda i=i: nc.vector.reduce_sum(
                            out=lamv[:, i:i + 1], in_=junk[:], axis=mybir.AxisListType.X)),
                            reads=[b_junk], writes=[b_lamv])
                    fw.op(ACT, lambda: nc.scalar.activation(out=lamv[:, 2:4], in_=lamv[:, 0:2], func=AF.Exp),
                          reads=[b_lamv], writes=[b_lamv])
                    fw.op(DVE, lambda: nc.vector.scalar_tensor_tensor(
                        out=lamv[:, 4:5], in0=lamv[:, 2:3], scalar=lam_init, in1=lamv[:, 3:4],
                        op0=ALU.add, op1=ALU.subtract), reads=[b_lamv], writes=[b_lamv])
                    fw.op(DVE, lambda: nc.vector.tensor_scalar(
                        out=lamv[:, 5:6], in0=lamv[:, 4:5], scalar1=-1.0, scalar2=None, op0=ALU.mult),
                        reads=[b_lamv], writes=[b_lamv])
                    wsf, b_wsf = sb(ph, [128, 4, 128], F32, "wsf")
                    fw.dma(SP, [(wsf[:], g_ws[l].rearrange("h p q -> p h q"))], b_wsf, writes=[b_wsf])
                    fw.dma(SP, [(bsT[:], g_bs[l].rearrange("h p -> p h"))], b_bsT, writes=[b_bsT],
                           allow_slow_non_contiguous=True)
                    for hh in range(4):
                        bi = next_bank()
                        fw.op(PE, (lambda hh=hh, bi=bi: nc.tensor.transpose(
                            out=bank(bi, 128), in_=wsf[:, hh, :], identity=identf[:])),
                            reads=[b_wsf, b_identf], writes=[bankb[bi]])
                        fw.op(DVE, (lambda hh=hh, bi=bi: nc.vector.tensor_copy(out=wsT[:, hh, :], in_=bank(bi, 128))),
                              reads=[bankb[bi]], writes=[b_wsT])
                    cs, b_cs = sb(ph, [128, 2, 128], F32, "cs")
                    fwt, b_fwt = sb(ph, [128, 2, 64], F32, "fwt")
                    fw.dma(SP, [(cs[:, 0, :], c64), (cs[:, 1, :], s64)], b_cs, writes=[b_cs])
                    fw.dma(SP, [(fwt[:, pr, :], four_w[l, 2 * pr:2 * pr + 2].rearrange("g c e -> (g c) e"))
                                for pr in range(2)], b_fwt, writes=[b_fwt])
                    fw.op(DVE, lambda: nc.vector.memset(AB[:], 0.0), writes=[b_AB])
                    for pr in range(2):
                        bi = next_bank()
                        fw.op(PE, (lambda pr=pr, bi=bi: nc.tensor.matmul(
                            bank(bi)[:, 0:64], lhsT=cs[:, 0, :], rhs=fwt[:, pr, :], start=True, stop=True)),
                            reads=[b_cs, b_fwt], writes=[bankb[bi]], signal=False)
                        fw.op(PE, (lambda pr=pr, bi=bi: nc.tensor.matmul(
                            bank(bi)[:, 64:128], lhsT=cs[:, 1, :], rhs=fwt[:, pr, :], start=True, stop=True,
                            skip_group_check=True)),
                            reads=[b_cs, b_fwt], writes=[bankb[bi]])
                        for hf in range(2):
                            rs = slice(hf * 64, hf * 64 + 64)
                            fw.op(DVE, (lambda rs=rs, pr=pr, bi=bi, hf=hf: nc.vector.tensor_copy(
                                out=AB[rs, pr, hf * 128:(hf + 1) * 128], in_=bank(bi)[rs, 0:128])),
                                reads=[bankb[bi]], writes=[b_AB])
                    wff, b_wff = sb(ph, [128, 8, 256], F32, "wff")
                    wfT, b_wfT = sb(ph, [128, 2, 1024], BF16, "wfT")
                    fw.op(DVE, lambda: nc.vector.tensor_copy(out=wff[:], in_=WT[:, :, 1792:2048]),
                          reads=[b_WT], writes=[b_wff])
                    for j in range(8):
                        for pr in range(2):
                            bi = next_bank()
                            fw.op(PE, (lambda j=j, pr=pr, bi=bi: nc.tensor.transpose(
                                out=bank(bi, 128), in_=wff[:, j, pr * 128:(pr + 1) * 128], identity=identf[:])),
                                reads=[b_wff, b_identf], writes=[bankb[bi]])
                            fw.op(DVE, (lambda j=j, pr=pr, bi=bi: nc.vector.tensor_copy(
                                out=wfT[:, pr, j * 128:(j + 1) * 128], in_=bank(bi, 128))),
                                reads=[bankb[bi]], writes=[b_wfT])
                    for j in range(8):
                        bi = next_bank()
                        for pr in range(2):
                            fw.op(PE, (lambda j=j, pr=pr, bi=bi: nc.tensor.matmul(
                                bank(bi)[:, pr * 256:(pr + 1) * 256], lhsT=wfT[:, pr, j * 128:(j + 1) * 128],
                                rhs=AB[:, pr, :], start=True, stop=True, skip_group_check=True)),
                                reads=[b_wfT, b_AB], writes=[bankb[bi]], signal=(pr == 1))
                        src = bank(bi).rearrange("p (g t e) -> p g t e", g=4, t=2)
                        fw.op(DVE, (lambda j=j, src=src: nc.vector.tensor_copy(
                            out=WT[:, j, 512:768].rearrange("p (g e) -> p g e", g=4), in_=src[:, :, 0, :])),
                            reads=[bankb[bi]], writes=[b_WT])
                        fw.op(DVE, (lambda j=j, src=src: nc.vector.tensor_copy(
                            out=WT[:, j, 768:1024].rearrange("p (g e) -> p g e", g=4), in_=src[:, :, 1, :])),
                            reads=[bankb[bi]], writes=[b_WT])
                    fw.barrier()
                    fw.release([b_junk, b_wsf, b_cs, b_fwt, b_wff, b_wfT])
                chk('setup')

                src_i = (l - 1) % 2
                dst_i = l % 2

                def x_src(J, t):
                    if l == 0:
                        return J.x_in[t * 128:(t + 1) * 128, :], []
                    return J.xscr[src_i][t * 128:(t + 1) * 128, :], [J.b_xs[src_i][t]]

                def x_dst(J, t):
                    if l == depth - 1:
                        return J.y_out[t * 128:(t + 1) * 128, :], []
                    return J.xscr[dst_i][t * 128:(t + 1) * 128, :], [J.b_xs[dst_i][t]]

                for J in jobs:
                    with ExitStack() as ph:
                        NQ, tpb = J.NQ, J.tpb
                        ss_t, ss_b = ssm[J.modrow]
                        xt = [sb(ph, [128, 1024], F32, "xt") for _ in range(tpb)]
                        xn = [sb(ph, [128, 1024], F32, "xn") for _ in range(tpb)]
                        st6, b_st6 = sb(ph, [128, 2, 6], F32, "st6")
                        mv4, b_mv4 = sb(ph, [128, tpb, 2], F32, "mv4")
                        ve4, b_ve4 = sb(ph, [128, tpb], F32, "ve4")
                        rs4, b_rs4 = sb(ph, [128, tpb], F32, "rs4")
                        hT = [sb(ph, [128, 8, NQ], BF16, "hT") for _ in range(2)]
                        rc = [sb(ph, [128, 2, NQ], F32, "rc") for _ in range(1)] if J.sample else None
                        qraw = [sb(ph, [128, NQ], BF16, "qraw") for _ in range(2)]
                        t1 = [sb(ph, [128, NQ], F32, "t1") for _ in range(1)]
                        t2 = [sb(ph, [128, NQ], F32, "t2") for _ in range(1)]
                        fst = [sb(ph, [128, NQ], BF16, "fst") for _ in range(3)]
                        vst = [sb(ph, [128, tpb, 512], BF16, "vst") for _ in range(1)]
                        pqst = [sb(ph, [128, tpb, 512], BF16, "pqst") for _ in range(1)]
                        gast = [sb(ph, [128, tpb, 512], BF16, "gast") for _ in range(1)]
                        mxst = [sb(ph, [128, 2, NQ], BF16, "mxst") for _ in range(2)]
                        uv4s = [sb(ph, [128, tpb, 512], F32, "uv") for _ in range(2)]
                        sgm4s = [sb(ph, [128, tpb, 256], F32, "sgm") for _ in range(2)]
                        gst, b_gst = sb(ph, [128, 4, 6], F32, "gst")
                        gmvs = [sb(ph, [128, tpb * 4, 2], F32, "gmv") for _ in range(2)]
                        gve, b_gve = sb(ph, [128, tpb * 4], F32, "gve")
                        grss = [sb(ph, [128, tpb * 4], F32, "grs") for _ in range(2)]
                        vnbs = [sb(ph, [128, 256], BF16, "vnb") for _ in range(tpb)]
                        gmos = [sb(ph, [128, 256], F32, "gmo") for _ in range(tpb)]
                        kvst = [sb(ph, [128, 512], F32, "kvst") for _ in range(2)] if not J.sample else None
                        phase_bufs = [b_st6, b_mv4, b_ve4, b_rs4, b_gst, b_gve] + [b for lst in (uv4s, sgm4s, gmvs, grss, vnbs, gmos) for (_, b) in lst]
                        for lst in (xt, xn, hT, qraw, t1, t2, fst, vst, pqst, gast, mxst):
                            phase_bufs += [b for (_, b) in lst]
                        if rc:
                            phase_bufs += [b for (_, b) in rc]
                        if kvst:
                            phase_bufs += [b for (_, b) in kvst]
                        kv_rr = [0]
                        f_rr = [0]

                        if J.sample:
                            ckt = [sb(ph, [128, 2, 128], F32, "ckt") for _ in range(2)]
                            phase_bufs += [b for (_, b) in ckt]
                            for hh in range(H):
                                ct, cb_ = ckt[0]
                                fw.dma(SP, [(ct[:], ck[l, hh].rearrange("(t p) e -> p t e", p=128))], cb_, writes=[cb_])
                                ft, fb = fst[f_rr[0] % 3]; f_rr[0] += 1
                                for tt in range(2):
                                    bi = next_bank()
                                    fw.op(PE, (lambda ct=ct, tt=tt, bi=bi: nc.tensor.transpose(
                                        out=bank(bi, 128), in_=ct[:, tt, :], identity=identf[:])),
                                        reads=[cb_, b_identf], writes=[bankb[bi]])
                                    fw.op(ACT, (lambda ft=ft, tt=tt, bi=bi: nc.scalar.copy(
                                        out=ft[:, tt * 128:(tt + 1) * 128], in_=bank(bi, 128))),
                                        reads=[bankb[bi]], writes=[fb])
                                fw.dma(POOL, [(J.KT[hh][:, 0:NPAST], ft[:, 0:NPAST])], fb, reads=[fb], writes=[J.b_kpast])
                                ct, cb_ = ckt[1]
                                fw.dma(SP, [(ct[:], cv[l, hh].rearrange("(t p) e -> p t e", p=128))], cb_, writes=[cb_])
                                vt, vb = vst[0]
                                fw.op(DVE, (lambda ct=ct, vt=vt, hh=hh: nc.vector.tensor_copy(
                                    out=vt[:, 0:2, hh * 128:(hh + 1) * 128], in_=ct[:])),
                                    reads=[cb_], writes=[vb])
                            vt, vb = vst[0]
                            fw.dma(POOL, [(J.V[0:NPAST, :].rearrange("(t p) c -> p t c", p=128), vt[:, 0:2, :])],
                                   vb, reads=[vb], writes=[J.b_vpast])

                        def ln_stats(blk):
                            for tt in range(tpb):
                                t = blk * tpb + tt
                                x_t, x_b = xt[tt]
                                src, sdeps = x_src(J, t)
                                fw.dma(SP, [(x_t[:, 0:512], src[:, 0:512]), (x_t[:, 512:1024], src[:, 512:1024])],
                                       x_b, reads=sdeps, writes=[x_b])
                                for hf in range(2):
                                    fw.op(DVE, (lambda hf=hf, x_t=x_t: nc.vector.bn_stats(
                                        out=st6[:, hf, :], in_=x_t[:, hf * 512:(hf + 1) * 512])),
                                        reads=[x_b], writes=[b_st6])
                                fw.op(DVE, lambda: nc.vector.bn_aggr(out=mv4[:, tt, :], in_=st6[:].rearrange("p a b -> p (a b)")),
                                      reads=[b_st6], writes=[b_mv4])
                            fw.op(DVE, lambda: nc.vector.tensor_scalar(
                                out=ve4[:], in0=mv4[:, :, 1], scalar1=LN_EPS, scalar2=None, op0=ALU.add),
                                reads=[b_mv4], writes=[b_ve4])
                            fw.op(POOL, lambda: nc.gpsimd.tensor_tensor(out=rs4[:], in0=ve4[:], in1=nhalf[:, 0:tpb], op=ALU.pow),
                                  reads=[b_ve4, b_nh], writes=[b_rs4])

                            for tt in range(tpb):
                                x_t, x_b = xt[tt]
                                xn_t, xn_b = xn[tt]
                                fw.op(DVE, (lambda x_t=x_t, xn_t=xn_t: nc.vector.tensor_scalar(
                                    out=xn_t[:], in0=x_t[:], scalar1=mv4[:, tt, 0:1], scalar2=rs4[:, tt:tt + 1],
                                    op0=ALU.subtract, op1=ALU.mult)), reads=[x_b, b_mv4, b_rs4], writes=[xn_b])

                        def ln_apply(blk):
                            hT_t, hT_b = hT[blk % 2]
                            for tt in range(tpb):
                                xn_t, xn_b = xn[tt]
                                for half in range(2):
                                    for jj in range(4):
                                        j = half * 4 + jj
                                        fw.op(PE, (lambda j=j, jj=jj, half=half, xn_t=xn_t: nc.tensor.transpose(
                                            out=ps[:, half * 512 + jj * 128: half * 512 + (jj + 1) * 128],
                                            in_=xn_t[:, j * 128:(j + 1) * 128], identity=identf[:])),
                                            reads=[xn_b, b_identf], writes=[bankb[half]], signal=(jj == 3))
                                    for jj in range(4):
                                        j = half * 4 + jj
                                        if jj % 2 == 0:
                                            fw.op(ACT, (lambda j=j, jj=jj, half=half, tt=tt, hT_t=hT_t: nc.scalar.activation(
                                                out=hT_t[:, j, tt * 128:(tt + 1) * 128],
                                                in_=ps[:, half * 512 + jj * 128: half * 512 + (jj + 1) * 128],
                                                func=AF.Identity, bias=ss_t[:, 0, j:j + 1], scale=ss_t[:, 1, j:j + 1])),
                                                reads=[bankb[half], ss_b], writes=[hT_b])
                                        else:
                                            fw.op(DVE, (lambda j=j, jj=jj, half=half, tt=tt, hT_t=hT_t: nc.vector.tensor_scalar(
                                                out=hT_t[:, j, tt * 128:(tt + 1) * 128],
                                                in0=ps[:, half * 512 + jj * 128: half * 512 + (jj + 1) * 128],
                                                scalar1=ss_t[:, 1, j:j + 1], scalar2=ss_t[:, 0, j:j + 1],
                                                op0=ALU.mult, op1=ALU.add)),
                                                reads=[bankb[half], ss_b], writes=[hT_b])

                        def fm(blk):
                            hT_t, hT_b = hT[blk % 2]
                            if J.sample:
                                rc_t, rc_b = rc[0]
                                fw.dma(SP, [(rc_t[:, 0, :], rope_cos[:, blk * NQ:(blk + 1) * NQ]),
                                            (rc_t[:, 1, :], rope_sin[:, blk * NQ:(blk + 1) * NQ])], rc_b, writes=[rc_b])
                            pendr = []

                            def rope_finish():
                                (fc_, hh_, dstT_, c_off_, ft_, fb_, qr_t, qr_b) = pendr.pop(0)
                                t1_t, t1_b = t1[0]
                                t2_t, t2_b = t2[0]
                                b2 = next_bank()
                                fw.op(PE, lambda: nc.tensor.matmul(bank(b2, NQ), lhsT=pmb[:], rhs=qr_t[:], start=True, stop=True),
                                      reads=[b_pmb, qr_b], writes=[bankb[b2]])
                                fw.op(DVE, lambda: nc.vector.tensor_tensor(out=t1_t[:], in0=qr_t[:], in1=rc_t[:, 0, :], op=ALU.mult),
                                      reads=[qr_b, rc_b], writes=[t1_b])
                                fw.op(DVE, lambda: nc.vector.tensor_tensor(out=t2_t[:], in0=bank(b2, NQ), in1=rc_t[:, 1, :], op=ALU.mult),
                                      reads=[bankb[b2], rc_b], writes=[t2_b])
                                fw.op(DVE, lambda: nc.vector.tensor_tensor(out=ft_[:], in0=t1_t[:], in1=t2_t[:], op=ALU.add),
                                      reads=[t1_b, t2_b], writes=[fb_])
                                fw.dma(POOL, [(dstT_[hh_][:, c_off_:c_off_ + NQ], ft_[:])], fb_, reads=[fb_],
                                       writes=[J.b_qk[blk]])

                            for fc in [8, 9] + list(range(8)):
                                bi = next_bank()
                                for j in range(8):
                                    fw.op(PE, (lambda j=j, fc=fc, bi=bi, hT_t=hT_t: nc.tensor.matmul(
                                        bank(bi, NQ), lhsT=WF[:, j, fc * 128:(fc + 1) * 128], rhs=hT_t[:, j, :],
                                        start=(j == 0), stop=(j == 7))),
                                        reads=[b_WF, hT_b], writes=[bankb[bi]], signal=(j == 7))
                                if pendr:
                                    rope_finish()
                                if fc < 8:
                                    hh = fc % 4
                                    dstT = J.QT if fc < 4 else J.KT
                                    c_off = blk * NQ + (0 if fc < 4 else J.npast)
                                    ft, fb = fst[f_rr[0] % 3]; f_rr[0] += 1
                                    if J.sample:
                                        qr_t, qr_b = qraw[fc % 2]
                                        t1_t, t1_b = t1[0]
                                        t2_t, t2_b = t2[0]
                                        fw.op(ACT, (lambda qr_t=qr_t, bi=bi: nc.scalar.copy(out=qr_t[:], in_=bank(bi, NQ))),
                                              reads=[bankb[bi]], writes=[qr_b])
                                        pendr.append((fc, hh, dstT, c_off, ft, fb, qr_t, qr_b))
                                        continue
                                        b2 = next_bank()
                                        fw.op(PE, (lambda qr_t=qr_t, b2=b2: nc.tensor.matmul(
                                            bank(b2, NQ), lhsT=pmb[:], rhs=qr_t[:], start=True, stop=True)),
                                            reads=[b_pmb, qr_b], writes=[bankb[b2]])
                                        fw.op(DVE, (lambda qr_t=qr_t, t1_t=t1_t, rc_t=rc_t: nc.vector.tensor_tensor(
                                            out=t1_t[:], in0=qr_t[:], in1=rc_t[:, 0, :], op=ALU.mult)),
                                            reads=[qr_b, rc_b], writes=[t1_b])
                                        fw.op(DVE, (lambda t2_t=t2_t, b2=b2, rc_t=rc_t: nc.vector.tensor_tensor(
                                            out=t2_t[:], in0=bank(b2, NQ), in1=rc_t[:, 1, :], op=ALU.mult)),
                                            reads=[bankb[b2], rc_b], writes=[t2_b])
                                        fw.op(DVE, (lambda ft=ft, t1_t=t1_t, t2_t=t2_t: nc.vector.tensor_tensor(
                                            out=ft[:], in0=t1_t[:], in1=t2_t[:], op=ALU.add)),
                                            reads=[t1_b, t2_b], writes=[fb])
                                    else:
                                        fw.op(ACT, (lambda ft=ft, bi=bi: nc.scalar.copy(out=ft[:], in_=bank(bi, NQ))),
                                              reads=[bankb[bi]], writes=[fb])
                                    fw.dma(POOL, [(dstT[hh][:, c_off:c_off + NQ], ft[:])], fb, reads=[fb],
                                           writes=[J.b_qk[blk]])
                                else:
                                    ft, fb = fst[f_rr[0] % 3]; f_rr[0] += 1
                                    fw.op(ACT, (lambda ft=ft, bi=bi: nc.scalar.activation(
                                        out=ft[:], in_=bank(bi, NQ), func=AF.Silu)),
                                        reads=[bankb[bi]], writes=[fb])
                                    cc = fc - 8
                                    fw.dma(POOL, [(J.SGF[cc * 128:(cc + 1) * 128, blk * NQ:(blk + 1) * NQ], ft[:])], fb,
                                           reads=[fb], writes=[J.b_sgf[blk]])
                            while pendr:
                                rope_finish()

                        def tmaj(blk):
                            hT_t, hT_b = hT[blk % 2]
                            uv4, b_uv = uv4s[blk % 2]
                            sgm4, b_sgm = sgm4s[blk % 2]
                            gmv, b_gmv = gmvs[blk % 2]
                            grs, b_grs = grss[blk % 2]
                            pb = blk - 1
                            gstate = {}
                            v_t, v_b = vst[0]
                            pq_t, pq_b = pqst[0]
                            ga_t, ga_b = gast[0]
                            mx_t, mx_b = mxst[blk % 2]
                            for tt in range(tpb):
                                t = blk * tpb + tt
                                tok = slice(tt * 128, (tt + 1) * 128)

                                if pb >= 0:
                                    g1(pb, tt)

                                def tm(c0, width, bi, Wsrc=WT, Wb=b_WT):
                                    for j in range(8):
                                        fw.op(PE, (lambda j=j: nc.tensor.matmul(
                                            bank(bi, width), lhsT=hT_t[:, j, tok], rhs=Wsrc[:, j, c0:c0 + width],
                                            start=(j == 0), stop=(j == 7))),
                                            reads=[Wb, hT_b], writes=[bankb[bi]], signal=(j == 7))
                                bi = next_bank(); tm(0, 512, bi)
                                fw.op(ACT, (lambda bi=bi: nc.scalar.copy(out=v_t[:, tt, :], in_=bank(bi))),
                                      reads=[bankb[bi]], writes=[v_b])
                                if not J.sample:
                                    kt_, kb_ = kvst[kv_rr[0] % 2]; kv_rr[0] += 1
                                    fw.op(DVE, (lambda bi=bi, kt_=kt_: nc.vector.tensor_copy(out=kt_[:], in_=bank(bi))),
                                          reads=[bankb[bi]], writes=[kb_])
                                    fw.dma(POOL, [(nv_o[J.idx - 1, l][:, t * 128:(t + 1) * 128, :].rearrange("h t e -> t h e"),
                                                   kt_[:].rearrange("p (h e) -> p h e", h=4))], kb_, reads=[kb_])
                                    bi = next_bank(); tm(512, 512, bi, WF, b_WF)
                                    kt_, kb_ = kvst[kv_rr[0] % 2]; kv_rr[0] += 1
                                    fw.op(DVE, (lambda bi=bi, kt_=kt_: nc.vector.tensor_copy(out=kt_[:], in_=bank(bi))),
                                          reads=[bankb[bi]], writes=[kb_])
                                    fw.dma(POOL, [(nk_o[J.idx - 1, l][:, t * 128:(t + 1) * 128, :].rearrange("h t e -> t h e"),
                                                   kt_[:].rearrange("p (h e) -> p h e", h=4))], kb_, reads=[kb_])
                                bi = next_bank(); tm(512, 512, bi)
                                fw.op(ACT, (lambda bi=bi: nc.scalar.copy(out=pq_t[:, tt, :], in_=bank(bi))),
                                      reads=[bankb[bi]], writes=[pq_b])
                                bi = next_bank(); tm(2048, 512, bi)
                                fw.op(ACT, (lambda bi=bi: nc.scalar.activation(out=ga_t[:, tt, :], in_=bank(bi), func=AF.Silu)),
                                      reads=[bankb[bi]], writes=[ga_b])
                                b_uvm = next_bank(); tm(1024, 512, b_uvm)
                                b_gm = next_bank(); tm(1536, 256, b_gm)
                                fw.op(ACT, (lambda b_gm=b_gm: nc.scalar.activation(out=sgm4[:, tt, :], in_=bank(b_gm, 256), func=AF.Silu)),
                                      reads=[bankb[b_gm]], writes=[b_sgm])
                                fw.op(DVE, (lambda b_uvm=b_uvm: nc.vector.tensor_copy(out=uv4[:, tt, :], in_=bank(b_uvm))),
                                      reads=[bankb[b_uvm]], writes=[b_uv])
                                for hh in range(4):
                                    fw.op(DVE, (lambda hh=hh: nc.vector.bn_stats(
                                        out=gst[:, hh, :], in_=uv4[:, tt, 256 + hh * 64:256 + (hh + 1) * 64])),
                                        reads=[b_uv], writes=[b_gst])
                                for hh in range(4):
                                    fw.op(DVE, (lambda hh=hh: nc.vector.bn_aggr(out=gmv[:, tt * 4 + hh, :], in_=gst[:, hh, :])),
                                          reads=[b_gst], writes=[b_gmv])
                                if pb >= 0:
                                    if tt > 0:
                                        g4(pb, tt - 1, gstate)
                                    g2(pb, tt, gstate)
                                    g3(pb, tt, gstate)
                            if pb >= 0:
                                g4(pb, tpb - 1, gstate)
                                gstore(pb)
                            fw.op(DVE, lambda: nc.vector.tensor_scalar(
                                out=gve[:], in0=gmv[:, :, 1], scalar1=LN_EPS, scalar2=None, op0=ALU.add),
                                reads=[b_gmv], writes=[b_gve])
                            fw.op(POOL, lambda: nc.gpsimd.tensor_tensor(out=grs[:], in0=gve[:], in1=nhalf[:, 0:tpb * 4], op=ALU.pow),
                                  reads=[b_gve, b_nh], writes=[b_grs])
                            r0 = J.npast + blk * NQ
                            fw.dma(POOL, [(J.V[r0:r0 + NQ, :].rearrange("(t p) c -> p t c", p=128), v_t[:])], v_b,
                                   reads=[v_b], writes=[J.b_v[blk]])
                            fw.dma(POOL, [(J.PQ[blk * NQ:(blk + 1) * NQ, :].rearrange("(t p) c -> p t c", p=128), pq_t[:])],
                                   pq_b, reads=[pq_b], writes=[J.b_pq[blk]])
                            fw.dma(POOL, [(J.SGA[blk * NQ:(blk + 1) * NQ, :].rearrange("(t p) c -> p t c", p=128), ga_t[:])],
                                   ga_b, reads=[ga_b], writes=[J.b_sga[blk]])

                        def g1(blk, tt):
                            uv4, b_uv = uv4s[blk % 2]
                            gmv, b_gmv = gmvs[blk % 2]
                            grs, b_grs = grss[blk % 2]
                            vnb, b_vnb = vnbs[tt]
                            for hh in range(4):
                                fw.op(DVE, (lambda hh=hh: nc.vector.tensor_scalar(
                                    out=vnb[:, hh * 64:(hh + 1) * 64], in0=uv4[:, tt, 256 + hh * 64:256 + (hh + 1) * 64],
                                    scalar1=gmv[:, tt * 4 + hh, 0:1], scalar2=grs[:, tt * 4 + hh:tt * 4 + hh + 1],
                                    op0=ALU.subtract, op1=ALU.mult)),
                                    reads=[b_uv, b_gmv, b_grs], writes=[b_vnb])

                        def g2(blk, tt, st):
                            vnb, b_vnb = vnbs[tt]
                            b_s = next_bank()
                            st[('s', tt)] = b_s
                            for hh in range(4):
                                fw.op(PE, (lambda hh=hh, b_s=b_s: nc.tensor.matmul(
                                    bank(b_s)[:, hh * 64:(hh + 1) * 64], lhsT=wsT[:, hh, :],
                                    rhs=vnb[:, hh * 64:(hh + 1) * 64], start=True, stop=True, skip_group_check=True)),
                                    reads=[b_wsT, b_vnb], writes=[bankb[b_s]], signal=(hh == 3))

                        def g3(blk, tt, st):
                            uv4, b_uv = uv4s[blk % 2]
                            sgm4, b_sgm = sgm4s[blk % 2]
                            gmo, b_gmo = gmos[tt]
                            b_s = st[('s', tt)]
                            for hh in range(4):
                                fw.op(DVE, (lambda hh=hh, b_s=b_s: nc.vector.scalar_tensor_tensor(
                                    out=gmo[:, hh * 64:(hh + 1) * 64], in0=bank(b_s)[:, hh * 64:(hh + 1) * 64],
                                    scalar=bsT[:, hh:hh + 1], in1=uv4[:, tt, hh * 64:(hh + 1) * 64],
                                    op0=ALU.add, op1=ALU.mult)),
                                    reads=[bankb[b_s], b_bsT, b_uv], writes=[b_gmo])
                            fw.op(DVE, lambda: nc.vector.tensor_tensor(out=gmo[:], in0=gmo[:], in1=sgm4[:, tt, :], op=ALU.mult),
                                  reads=[b_gmo, b_sgm], writes=[b_gmo])

                        def g4(blk, tt, st):
                            mx_t, mx_b = mxst[blk % 2]
                            gmo, b_gmo = gmos[tt]
                            tok = slice(tt * 128, (tt + 1) * 128)
                            b_t = next_bank()
                            for cc in range(2):
                                fw.op(PE, (lambda cc=cc, b_t=b_t: nc.tensor.transpose(
                                    out=bank(b_t)[:, cc * 128:(cc + 1) * 128], in_=gmo[:, cc * 128:(cc + 1) * 128],
                                    identity=identf[:])),
                                    reads=[b_gmo, b_identf], writes=[bankb[b_t]], signal=(cc == 1))
                            fw.op(ACT, (lambda b_t=b_t: nc.scalar.copy(
                                out=mx_t[:, :, tok], in_=bank(b_t, 256).rearrange("p (c t) -> p c t", c=2))),
                                reads=[bankb[b_t]], writes=[mx_b])

                        def gstore(blk):
                            mx_t, mx_b = mxst[blk % 2]
                            fw.dma(POOL, [(J.MIXT[768:1024, blk * NQ:(blk + 1) * NQ].rearrange("(c p) t -> p c t", p=128),
                                           mx_t[:])], mx_b, reads=[mx_b], writes=[J.b_mix[blk]])

                        def gtail(blk):
                            st = {}
                            for tt in range(tpb):
                                g1(blk, tt)
                                g2(blk, tt, st)
                                g3(blk, tt, st)
                                g4(blk, tt, st)
                            gstore(blk)

                        ln_stats(0)
                        ln_apply(0)
                        for blk in range(J.nblk):
                            if blk + 1 < J.nblk:
                                ln_stats(blk + 1)
                            fm(blk)
                            if blk + 1 < J.nblk:
                                ln_apply(blk + 1)
                            tmaj(blk)
                        gtail(J.nblk - 1)
                        fw.barrier()
                        fw.release(phase_bufs)

                chk('p1')
                for J in jobs:
                    with ExitStack() as ph:
                        NQ, tpb, Lk = J.NQ, J.tpb, J.Lk
                        nkc = Lk // 128
                        KT0 = [sb(ph, [128, Lk], BF16, "KT0") for _ in range(2)]
                        KT1 = [sb(ph, [128, Lk], BF16, "KT1") for _ in range(2)]
                        QTs = [sb(ph, [128, J.L], BF16, "QTs") for _ in range(2)]
                        V1 = [sb(ph, [128, nkc, 132], BF16, "V1") for _ in range(2)]
                        NE = 3
                        Et = [sb(ph, [128, 2 * NQ], BF16, "E") for _ in range(NE)]
                        Oc, b_Oc = sb(ph, [128, 3 * 512], F32, "Oc")
                        sga_t = [sb(ph, [128, tpb, 128], BF16, "sga") for _ in range(2)]
                        rr, b_rr = sb(ph, [128, 4], F32, "rr")
                        ta, b_ta = sb(ph, [128, 128], F32, "ta")
                        to4, b_to = sb(ph, [128, tpb, 128], F32, "to")
                        ms4, b_ms = sb(ph, [128, tpb], F32, "ms4")
                        me4, b_me = sb(ph, [128, tpb], F32, "me4")
                        rq4, b_rq = sb(ph, [128, tpb], F32, "rq4")
                        tj, b_tj = sb(ph, [128, 128], F32, "tj")
                        tg = [sb(ph, [128, 128], F32, "tg") for _ in range(tpb)]
                        mst = [sb(ph, [128, NQ], BF16, "mst") for _ in range(2)]
                        phase_bufs = [b_Oc, b_rr, b_ta, b_to, b_tj, b_ms, b_me, b_rq]
                        for lst in (KT0, KT1, QTs, V1, Et, sga_t, tg, mst):
                            phase_bufs += [b for (_, b) in lst]
                        for i in range(2):
                            fw.op(POOL, (lambda i=i: nc.gpsimd.memset(KT0[i][0][64:128, :], 0.0)), writes=[KT0[i][1]])
                            fw.op(POOL, (lambda i=i: nc.gpsimd.memset(KT1[i][0][0:64, :], 0.0)), writes=[KT1[i][1]])
                            fw.op(POOL, (lambda i=i: nc.gpsimd.memset(V1[i][0][:, :, 128:132], 1.0)), writes=[V1[i][1]])
                        qk_deps = list(J.b_qk) + ([J.b_kpast] if J.sample else [])
                        v_deps = list(J.b_v) + ([J.b_vpast] if J.sample else [])
                        nreg = 2 * tpb

                        def oreg(m, sub):
                            idx = m * tpb + sub
                            bk = 4 + idx // 3
                            off = (idx % 3) * 129
                            return bk, off, idx
                        e_rr = 0
                        g_rr = [0]
                        sg_rr = [0]
                        pending = []
                        pending2 = []
                        for hh in range(H):
                            k0_t, k0_b = KT0[hh % 2]
                            k1_t, k1_b = KT1[hh % 2]
                            q_t, q_b = QTs[hh % 2]
                            v1_t, v1_b = V1[hh % 2]
                            fw.dma(SP, [(k0_t[0:64, :], J.KT[hh][0:64, :])], k0_b, reads=qk_deps, writes=[k0_b])
                            fw.dma(SP, [(k1_t[64:128, :], J.KT[hh][64:128, :])], k1_b, reads=qk_deps, writes=[k1_b])
                            fw.dma(SP, [(q_t[:], J.QT[hh])], q_b, reads=qk_deps, writes=[q_b])
                            fw.dma(SP, [(v1_t[:, :, 0:128],
                                         J.V[:, hh * 128:(hh + 1) * 128].rearrange("(k p) e -> p k e", p=128))],
                                   v1_b, reads=v_deps, writes=[v1_b])
                            for qb in range(J.nblk):
                                sg_t, sg_b = sga_t[sg_rr[0] % 2]; sg_rr[0] += 1
                                fw.dma(SP, [(sg_t[:], J.SGA[qb * NQ:(qb + 1) * NQ, hh * 128:(hh + 1) * 128]
                                             .rearrange("(t p) e -> p t e", p=128))], sg_b, reads=[J.b_sga[qb]], writes=[sg_b])
                                qs = slice(qb * NQ, (qb + 1) * NQ)

                                def qk(kc):
                                    sbk = (kc % 2) * 2
                                    ks = slice(kc * 128, (kc + 1) * 128)
                                    fw.op(PE, lambda: nc.tensor.matmul(bank(sbk, NQ), lhsT=k0_t[0:64, ks], rhs=q_t[0:64, qs],
                                                                       start=True, stop=True),
                                          reads=[k0_b, q_b], writes=[bankb[sbk]], signal=False)
                                    fw.op(PE, lambda: nc.tensor.matmul(bank(sbk + 1, NQ), lhsT=k1_t[64:128, ks], rhs=q_t[64:128, qs],
                                                                       start=True, stop=True),
                                          reads=[k1_b, q_b], writes=[bankb[sbk], bankb[sbk + 1]])
                                qk(0)
                                if nkc > 1:
                                    qk(1)
                                for kc in range(nkc):
                                    if kc == min(24, nkc - 1) and pending:
                                        pending.pop(0)()
                                    if kc == min(27, nkc - 1) and pending2:
                                        pending2.pop(0)()
                                    sbk = (kc % 2) * 2
                                    e_t, e_b = Et[e_rr % NE]; e_rr += 1
                                    src = ps[:, sbk * 512:(sbk + 2) * 512].rearrange("p (m c) -> p m c", m=2)[:, :, 0:NQ]
                                    fw.op(ACT, (lambda e_t=e_t, src=src: nc.scalar.activation(
                                        out=e_t[:].rearrange("p (m c) -> p m c", m=2), in_=src, func=AF.Exp, scale=0.125)),
                                        reads=[bankb[sbk], bankb[sbk + 1]], writes=[e_b])
                                    if kc + 2 < nkc:
                                        qk(kc + 2)
                                    started = set()
                                    for m in range(2):
                                        for sub in range(tpb):
                                            bk, off, idx = oreg(m, sub)
                                            first = (kc == 0 and bk not in started)
                                            started.add(bk)
                                            last = (m == 1 and sub == tpb - 1)
                                            fw.op(PE, (lambda m=m, sub=sub, bk=bk, off=off, first=first, e_t=e_t: nc.tensor.matmul(
                                                bank(bk)[:, off:off + 129],
                                                lhsT=e_t[:, m * NQ + sub * 128: m * NQ + (sub + 1) * 128],
                                                rhs=v1_t[:, kc, 0:129], start=first, stop=(kc == nkc - 1),
                                                skip_group_check=True)),
                                                reads=[e_b, v1_b], writes=[bankb[bk]],
                                                signal=last)
                                fw.op(DVE, lambda: nc.vector.tensor_copy(out=Oc[:], in_=ps[:, 4 * 512:7 * 512]),
                                      reads=[bankb[4], bankb[5], bankb[6]], writes=[b_Oc])
                                def post(hh=hh, qb=qb, sg_t=sg_t, sg_b=sg_b, qs=qs):
                                    m_t, m_b = mst[qb % 2]
                                    for sub in range(tpb):
                                        _, _, i0 = oreg(0, sub)
                                        _, _, i1 = oreg(1, sub)
                                        o0 = (i0 // 3) * 512 + (i0 % 3) * 129
                                        o1 = (i1 // 3) * 512 + (i1 % 3) * 129
                                        fw.op(DVE, lambda: nc.vector.reciprocal(out=rr[:, 0:1], in_=Oc[:, o0 + 128:o0 + 129]),
                                              reads=[b_Oc], writes=[b_rr])
                                        fw.op(DVE, lambda: nc.vector.reciprocal(out=rr[:, 1:2], in_=Oc[:, o1 + 128:o1 + 129]),
                                              reads=[b_Oc], writes=[b_rr])
                                        fw.op(DVE, lambda: nc.vector.tensor_tensor(out=rr[:, 1:2], in0=rr[:, 1:2], in1=lamv[:, 5:6],
                                                                                   op=ALU.mult),
                                              reads=[b_rr, b_lamv], writes=[b_rr])
                                        fw.op(DVE, lambda: nc.vector.tensor_scalar(out=ta[:], in0=Oc[:, o1:o1 + 128],
                                                                                   scalar1=rr[:, 1:2], scalar2=None, op0=ALU.mult),
                                              reads=[b_Oc, b_rr], writes=[b_ta])
                                        fw.op(DVE, lambda: nc.vector.scalar_tensor_tensor(
                                            out=to4[:, sub, :], in0=Oc[:, o0:o0 + 128], scalar=rr[:, 0:1], in1=ta[:],
                                            op0=ALU.mult, op1=ALU.add), reads=[b_Oc, b_rr, b_ta], writes=[b_to])
                                        fw.op(DVE, lambda: nc.vector.tensor_tensor(out=tj[:], in0=to4[:, sub, :], in1=to4[:, sub, :],
                                                                                   op=ALU.mult),
                                              reads=[b_to], writes=[b_tj])
                                        fw.op(DVE, lambda: nc.vector.reduce_sum(out=ms4[:, sub:sub + 1], in_=tj[:],
                                                                                axis=mybir.AxisListType.X),
                                              reads=[b_tj], writes=[b_ms])
                                    fw.op(DVE, lambda: nc.vector.tensor_scalar(
                                        out=me4[:], in0=ms4[:], scalar1=1.0 / 128.0, scalar2=RMS_EPS,
                                        op0=ALU.mult, op1=ALU.add), reads=[b_ms], writes=[b_me])
                                    fw.op(POOL, lambda: nc.gpsimd.tensor_tensor(out=rq4[:], in0=me4[:], in1=nhalf[:, 0:tpb], op=ALU.pow),
                                          reads=[b_me, b_nh], writes=[b_rq])
                                    for sub in range(tpb):
                                        fw.op(DVE, lambda: nc.vector.scalar_tensor_tensor(
                                            out=ta[:], in0=to4[:, sub, :], scalar=rq4[:, sub:sub + 1], in1=SW[:],
                                            op0=ALU.mult, op1=ALU.mult),
                                            reads=[b_to, b_rq, b_SW], writes=[b_ta])
                                        g_t, g_b = tg[sub]
                                        fw.op(DVE, (lambda g_t=g_t, sub=sub: nc.vector.tensor_tensor(
                                            out=g_t[:], in0=ta[:], in1=sg_t[:, sub, :], op=ALU.mult)),
                                            reads=[b_ta, sg_b], writes=[g_b])
                                    for sub in range(tpb):
                                        g_t, g_b = tg[sub]
                                        fw.op(PE, (lambda g_t=g_t, sub=sub: nc.tensor.transpose(
                                            out=bank(7)[:, sub * 128:(sub + 1) * 128], in_=g_t[:], identity=identf[:])),
                                            reads=[g_b, b_identf], writes=[bankb[7]], signal=(sub == tpb - 1))
                                    def postB(m_t=m_t, m_b=m_b, hh=hh, qs=qs, qb=qb):
                                        fw.op(ACT, (lambda m_t=m_t: nc.scalar.copy(out=m_t[:, 0:tpb * 128], in_=bank(7, tpb * 128))),
                                              reads=[bankb[7]], writes=[m_b])
                                        fw.dma(POOL, [(J.MIXT[hh * 128:(hh + 1) * 128, qs], m_t[:])], m_b, reads=[m_b],
                                               writes=[J.b_mix[qb]])
                                    pending2.append(postB)
                                pending.append(post)
                        while pending:
                            pending.pop(0)()
                        while pending2:
                            pending2.pop(0)()
                        fw.barrier()
                        fw.release(phase_bufs)

                chk('p2')
                for J in jobs:
                    with ExitStack() as ph:
                        NQ = J.NQ
                        nlc = J.L // 128
                        GL = 4 if J.sample else 2
                        tab_c = dfts_c if J.sample else dftp_c
                        tab_s = dfts_s if J.sample else dftp_s
                        pq_sb, b_pqs = sb(ph, [128, nlc, 512], BF16, "pqsb")
                        tc_ = [sb(ph, [128, GL, NQ], BF16, "tabc") for _ in range(2)]
                        ts_ = [sb(ph, [128, GL, NQ], BF16, "tabs") for _ in range(2)]
                        sgf_t = [sb(ph, [128, NQ], BF16, "sgft") for _ in range(2)]
                        fo = [sb(ph, [128, NQ], BF16, "fo") for _ in range(2)]
                        phase_bufs = [b_pqs] + [b for lst in (tc_, ts_, sgf_t, fo) for (_, b) in lst]
                        fw.dma(SP, [(pq_sb[:], J.PQ.rearrange("(k p) c -> p k c", p=128))], b_pqs, reads=list(J.b_pq),
                               writes=[b_pqs])
                        fscale = 1.0 / math.sqrt(64.0 * J.L)
                        gi = 0
                        oi = 0
                        use_sym = J.sample and J.nblk >= 2
                        if use_sym:
                            L = J.L
                            wsb = [sb(ph, [128, 512], F32, "wsb") for _ in range(2)]
                            at_ = [sb(ph, [128, 512], F32, "fa") for _ in range(2)]
                            bt_ = [sb(ph, [128, 512], F32, "fb") for _ in range(2)]
                            sgh = [sb(ph, [128, 512], BF16, "sgh") for _ in range(2)]
                            fh = [sb(ph, [128, 512], BF16, "fh") for _ in range(2)]
                            fn_, b_fn = sb(ph, [128, 4], BF16, "fn")
                            sgn, b_sgn = sb(ph, [128, 4], BF16, "sgn")
                            phase_bufs += [b_fn, b_sgn] + [b for lst in (wsb, at_, bt_, sgh, fh) for (_, b) in lst]
                            for kb in range(J.nblk // 2):
                                bU = [next_bank(), next_bank()]
                                bW = [next_bank(), next_bank()]
                                for g0 in range(0, nlc, GL):
                                    c_t, c_b = tc_[gi % 2]
                                    s_t, s_b = ts_[gi % 2]
                                    gi += 1
                                    fw.dma(SP, [(c_t[:], tab_c[kb][:, g0:g0 + GL, :])], c_b, writes=[c_b])
                                    fw.dma(SP, [(s_t[:], tab_s[kb][:, g0:g0 + GL, :])], s_b, writes=[s_b])
                                    for gl in range(GL):
                                        lc = g0 + gl
                                        for cc in range(2):
                                            fw.op(PE, lambda: nc.tensor.matmul(
                                                bank(bU[cc]), lhsT=pq_sb[:, lc, cc * 128:(cc + 1) * 128], rhs=c_t[:, gl, :],
                                                start=(lc == 0), stop=(lc == nlc - 1)),
                                                reads=[b_pqs, c_b], writes=[bankb[bU[cc]]], signal=False)
                                            fw.op(PE, lambda: nc.tensor.matmul(
                                                bank(bW[cc]), lhsT=pq_sb[:, lc, 256 + cc * 128:256 + (cc + 1) * 128],
                                                rhs=s_t[:, gl, :], start=(lc == 0), stop=(lc == nlc - 1)),
                                                reads=[b_pqs, s_b], writes=[bankb[bU[cc]], bankb[bW[cc]]],
                                                signal=(gl == GL - 1))
                                k0h = L - kb * 512 - 511
                                nh_ = 511 if kb == 0 else 512
                                hb = sorted(set([k0h // 512, (k0h + nh_ - 1) // 512]))
                                for cc in range(2):
                                    w_t, w_b = wsb[oi % 2]
                                    a_t, a_b = at_[oi % 2]
                                    b_t, b_b = bt_[oi % 2]
                                    sg_t, sg_b = sgf_t[oi % 2]
                                    sh_t, sh_b = sgh[oi % 2]
                                    f_t, f_b = fo[oi % 2]
                                    h_t, h_b = fh[oi % 2]
                                    oi += 1
                                    rows = slice(cc * 128, (cc + 1) * 128)
                                    fw.dma(SP, [(sg_t[:], J.SGF[rows, kb * 512:(kb + 1) * 512])], sg_b,
                                           reads=[J.b_sgf[kb]], writes=[sg_b])
                                    fw.dma(SP, [(sh_t[:, 0:nh_], J.SGF[rows, k0h:k0h + nh_])], sh_b,
                                           reads=[J.b_sgf[i] for i in hb], writes=[sh_b])
                                    fw.op(ACT, lambda: nc.scalar.copy(out=w_t[:], in_=bank(bW[cc])),
                                          reads=[bankb[bW[cc]]], writes=[w_b])
                                    fw.op(DVE, lambda: nc.vector.tensor_tensor(out=a_t[:], in0=bank(bU[cc]), in1=w_t[:], op=ALU.add),
                                          reads=[bankb[bU[cc]], w_b], writes=[a_b])
                                    fw.op(DVE, lambda: nc.vector.tensor_tensor(out=b_t[:], in0=bank(bU[cc]), in1=w_t[:], op=ALU.subtract),
                                          reads=[bankb[bU[cc]], w_b], writes=[b_b])
                                    fw.op(DVE, lambda: nc.vector.scalar_tensor_tensor(
                                        out=f_t[:], in0=a_t[:], scalar=fscale, in1=sg_t[:], op0=ALU.mult, op1=ALU.mult),
                                        reads=[a_b, sg_b], writes=[f_b])
                                    fw.op(DVE, lambda: nc.vector.scalar_tensor_tensor(
                                        out=h_t[:, 0:nh_], in0=b_t[:, ::-1][:, 0:nh_], scalar=fscale, in1=sh_t[:, 0:nh_],
                                        op0=ALU.mult, op1=ALU.mult),
                                        reads=[b_b, sh_b], writes=[h_b])
                                    fw.dma(POOL, [(J.MIXT[512 + cc * 128:512 + (cc + 1) * 128, kb * 512:(kb + 1) * 512], f_t[:])],
                                           f_b, reads=[f_b], writes=[J.b_mix[kb]])
                                    fw.dma(POOL, [(J.MIXT[512 + cc * 128:512 + (cc + 1) * 128, k0h:k0h + nh_], h_t[:, 0:nh_])],
                                           h_b, reads=[h_b], writes=[J.b_mix[i] for i in hb])
                            kN = L // 2
                            fw.dma(SP, [(sgn[:, cc:cc + 1], J.SGF[cc * 128:(cc + 1) * 128, kN:kN + 1]) for cc in range(2)],
                                   b_sgn, reads=[J.b_sgf[kN // 512]], writes=[b_sgn], allow_slow_non_contiguous=True)
                            for cc in range(2):
                                for lc in range(nlc):
                                    fw.op(PE, lambda: nc.tensor.matmul(
                                        bank(cc)[:, 0:2], lhsT=pq_sb[:, lc, cc * 128:(cc + 1) * 128], rhs=altt[:, 0:2],
                                        start=(lc == 0), stop=(lc == nlc - 1)),
                                        reads=[b_pqs, b_alt], writes=[bankb[cc]], signal=(lc == nlc - 1))
                                fw.op(DVE, lambda: nc.vector.scalar_tensor_tensor(
                                    out=fn_[:, cc:cc + 1], in0=bank(cc)[:, 0:1], scalar=fscale, in1=sgn[:, cc:cc + 1],
                                    op0=ALU.mult, op1=ALU.mult),
                                    reads=[bankb[cc], b_sgn], writes=[b_fn])
                            fw.dma(POOL, [(J.MIXT[512 + cc * 128:512 + (cc + 1) * 128, kN:kN + 1], fn_[:, cc:cc + 1]) for cc in range(2)],
                                   b_fn, reads=[b_fn], writes=[J.b_mix[kN // 512]], allow_slow_non_contiguous=True)
                        for kb in range(0 if use_sym else J.nblk):
                            bks = [next_bank(), next_bank()]
                            for g0 in range(0, nlc, GL):
                                c_t, c_b = tc_[gi % 2]
                                s_t, s_b = ts_[gi % 2]
                                gi += 1
                                fw.dma(SP, [(c_t[:], tab_c[kb][:, g0:g0 + GL, :])], c_b, writes=[c_b])
                                fw.dma(SP, [(s_t[:], tab_s[kb][:, g0:g0 + GL, :])], s_b, writes=[s_b])
                                for gl in range(GL):
                                    lc = g0 + gl
                                    for cc in range(2):
                                        fw.op(PE, (lambda cc=cc, lc=lc, gl=gl, c_t=c_t: nc.tensor.matmul(
                                            bank(bks[cc], NQ), lhsT=pq_sb[:, lc, cc * 128:(cc + 1) * 128], rhs=c_t[:, gl, :],
                                            start=(lc == 0), stop=False)),
                                            reads=[b_pqs, c_b], writes=[bankb[bks[cc]]], signal=False)
                                        lastmm = (lc == nlc - 1)
                                        fw.op(PE, (lambda cc=cc, lc=lc, gl=gl, s_t=s_t, lastmm=lastmm: nc.tensor.matmul(
                                            bank(bks[cc], NQ), lhsT=pq_sb[:, lc, 256 + cc * 128:256 + (cc + 1) * 128],
                                            rhs=s_t[:, gl, :], start=False, stop=lastmm)),
                                            reads=[b_pqs, s_b], writes=[bankb[bks[cc]]],
                                            signal=(gl == GL - 1))
                            for cc in range(2):
                                sg_t, sg_b = sgf_t[oi % 2]
                                f_t, f_b = fo[oi % 2]
                                oi += 1
                                fw.dma(SP, [(sg_t[:], J.SGF[cc * 128:(cc + 1) * 128, kb * NQ:(kb + 1) * NQ])], sg_b,
                                       reads=[J.b_sgf[kb]], writes=[sg_b])
                                fw.op(DVE, (lambda cc=cc, f_t=f_t, sg_t=sg_t: nc.vector.scalar_tensor_tensor(
                                    out=f_t[:], in0=bank(bks[cc], NQ), scalar=fscale, in1=sg_t[:],
                                    op0=ALU.mult, op1=ALU.mult)),
                                    reads=[bankb[bks[cc]], sg_b], writes=[f_b])
                                fw.dma(POOL, [(J.MIXT[512 + cc * 128:512 + (cc + 1) * 128, kb * NQ:(kb + 1) * NQ], f_t[:])],
                                       f_b, reads=[f_b], writes=[J.b_mix[kb]])
                        fw.barrier()
                        fw.release(phase_bufs)

                chk('p3')
                for J in jobs:
                    with ExitStack() as ph:
                        NQ, tpb = J.NQ, J.tpb
                        G_t, G_b = Gt[J.modrow]
                        mx = [sb(ph, [128, 8, NQ], BF16, "mx") for _ in range(2)]
                        xr = [sb(ph, [128, 1024], F32, "xr") for _ in range(2)]
                        ty = [sb(ph, [128, 1024], F32, "ty") for _ in range(2 * tpb)]
                        yo = [sb(ph, [128, 1024], F32, "yo") for _ in range(3)]
                        st6, b_st6 = sb(ph, [128, 2, 6], F32, "st6")
                        mv4s = [sb(ph, [128, tpb, 2], F32, "mv4") for _ in range(2)]
                        ve4s = [sb(ph, [128, tpb], F32, "ve4") for _ in range(2)]
                        rs4s = [sb(ph, [128, tpb], F32, "rs4") for _ in range(2)]
                        phase_bufs = [b_st6] + [b for lst in (mx, xr, ty, yo, mv4s, ve4s, rs4s) for (_, b) in lst]

                        WOg, b_WOg = sb(ph, [128, 8, 1024], BF16, "WOg")
                        nb4s = [sb(ph, [128, tpb], F32, "nb4") for _ in range(2)]
                        phase_bufs += [b_WOg] + [b for (_, b) in nb4s]
                        for j in range(8):
                            fw.op(DVE, (lambda j=j: nc.vector.tensor_tensor(out=WOg[:, j, :], in0=WO[:, j, :], in1=G_t[:], op=ALU.mult)),
                                  reads=[b_WO, G_b], writes=[b_WOg])

                        junk5, b_junk5 = sb(ph, [128, 1024], F32, "junk5")
                        s12s = [sb(ph, [128, 2, tpb], F32, "s12") for _ in range(2)]
                        phase_bufs += [b_junk5] + [b for (_, b) in s12s]

                        def stageA(blk):
                            mx_t, mx_b = mx[blk % 2]
                            mv4, b_mv4 = mv4s[blk % 2]
                            s12, b_s12 = s12s[blk % 2]
                            fw.dma(SP, [(mx_t[:], J.MIXT[:, blk * NQ:(blk + 1) * NQ].rearrange("(c p) t -> p c t", p=128))],
                                   mx_b, reads=[J.b_mix[blk]], writes=[mx_b])
                            for tt in range(tpb):
                                t = blk * tpb + tt
                                tok = slice(tt * 128, (tt + 1) * 128)
                                x_t, x_b = xr[t % 2]
                                y_t, y_b = ty[(blk % 2) * tpb + tt]
                                src, sdeps = x_src(J, t)
                                fw.dma(SP, [(x_t[:], src)], x_b, reads=sdeps, writes=[x_b])
                                bks = [next_bank(), next_bank()]
                                for nb in range(2):
                                    for j in range(8):
                                        fw.op(PE, (lambda j=j, nb=nb: nc.tensor.matmul(
                                            bank(bks[nb]), lhsT=mx_t[:, j, tok], rhs=WOg[:, j, nb * 512:(nb + 1) * 512],
                                            start=(j == 0), stop=(j == 7))),
                                            reads=[mx_b, b_WOg], writes=[bankb[bks[nb]]], signal=(j == 7))
                                for nb in range(2):
                                    cs_ = slice(nb * 512, (nb + 1) * 512)
                                    fw.op(DVE, (lambda nb=nb, cs_=cs_, y_t=y_t, x_t=x_t: nc.vector.scalar_tensor_tensor(
                                        out=y_t[:, cs_], in0=x_t[:, cs_], scalar=ALPHA, in1=bank(bks[nb]),
                                        op0=ALU.mult, op1=ALU.add)),
                                        reads=[bankb[bks[nb]], x_b], writes=[y_b])
                                fw.op(ACT, (lambda y_t=y_t: nc.scalar.activation(
                                    out=junk5[:], in_=y_t[:], func=AF.Identity, accum_out=s12[:, 0, tt:tt + 1])),
                                    reads=[y_b], writes=[b_junk5, b_s12])
                                fw.op(ACT, (lambda y_t=y_t: nc.scalar.activation(
                                    out=junk5[:], in_=y_t[:], func=AF.Square, accum_out=s12[:, 1, tt:tt + 1])),
                                    reads=[y_b], writes=[b_junk5, b_s12])
                            ve4, b_ve4 = ve4s[blk % 2]
                            rs4, b_rs4 = rs4s[blk % 2]
                            fw.op(DVE, lambda: nc.vector.tensor_scalar(
                                out=mv4[:, :, 0], in0=s12[:, 0, :], scalar1=1.0 / D, scalar2=None, op0=ALU.mult),
                                reads=[b_s12], writes=[b_mv4])
                            fw.op(DVE, lambda: nc.vector.tensor_tensor(out=mv4[:, :, 1], in0=mv4[:, :, 0], in1=mv4[:, :, 0], op=ALU.mult),
                                  reads=[b_mv4], writes=[b_mv4])
                            fw.op(DVE, lambda: nc.vector.scalar_tensor_tensor(
                                out=mv4[:, :, 1], in0=s12[:, 1, :], scalar=1.0 / D, in1=mv4[:, :, 1], op0=ALU.mult, op1=ALU.subtract),
                                reads=[b_s12, b_mv4], writes=[b_mv4])
                            fw.op(DVE, lambda: nc.vector.tensor_scalar(
                                out=ve4[:], in0=mv4[:, :, 1], scalar1=LN_EPS, scalar2=None, op0=ALU.add),
                                reads=[b_mv4], writes=[b_ve4])
                            fw.op(POOL, lambda: nc.gpsimd.tensor_tensor(out=rs4[:], in0=ve4[:], in1=nhalf[:, 0:tpb], op=ALU.pow),
                                  reads=[b_ve4, b_nh], writes=[b_rs4])
                            nb4, b_nb4 = nb4s[blk % 2]
                            fw.op(DVE, lambda: nc.vector.scalar_tensor_tensor(
                                out=nb4[:], in0=mv4[:, :, 0], scalar=-1.0, in1=rs4[:], op0=ALU.mult, op1=ALU.mult),
                                reads=[b_mv4, b_rs4], writes=[b_nb4])

                        def stageB(blk):
                            mv4, b_mv4 = mv4s[blk % 2]
                            rs4, b_rs4 = rs4s[blk % 2]
                            nb4, b_nb4 = nb4s[blk % 2]
                            for tt in range(tpb):
                                t = blk * tpb + tt
                                y_t, y_b = ty[(blk % 2) * tpb + tt]
                                o_t, o_b = yo[t % 3]
                                fw.op(ACT, (lambda y_t=y_t, o_t=o_t: nc.scalar.activation(
                                    out=o_t[:], in_=y_t[:], func=AF.Identity, bias=nb4[:, tt:tt + 1], scale=rs4[:, tt:tt + 1])),
                                    reads=[y_b, b_nb4, b_rs4], writes=[o_b])
                                fw.op(DVE, (lambda o_t=o_t: nc.vector.tensor_tensor(out=o_t[:], in0=o_t[:], in1=lnG[:], op=ALU.mult)),
                                      reads=[o_b, b_lnG], writes=[o_b])
                                fw.op(DVE, (lambda o_t=o_t: nc.vector.tensor_tensor(out=o_t[:], in0=o_t[:], in1=lnB[:], op=ALU.add)),
                                      reads=[o_b, b_lnB], writes=[o_b])
                                dst, ddeps = x_dst(J, t)
                                fw.dma(POOL, [(dst, o_t[:])], o_b, reads=[o_b], writes=ddeps)

                        stageA(0)
                        for blk in range(J.nblk):
                            if blk + 1 < J.nblk:
                                stageA(blk + 1)
                            stageB(blk)
                        fw.barrier()
                        fw.release(phase_bufs)

        except StopBuild:
            for e_ in fw.engs:
                e_.pending = False

        fw.barrier()
    return nc


_CACHE = {}


def _get_program():
    if "nc" not in _CACHE:
        _CACHE["nc"] = build_program()
        _CACHE["consts"] = _consts(4096, 256)
    return _CACHE["nc"], _CACHE["consts"]


def kernel(x_prompt, x_sample, c, cache_k, cache_v, c_ctx, w_ada, b_ada, w_in, w_out,
           lam_q1, lam_k1, lam_q2, lam_k2, subln_w, fourier_w, gmlp_ws, gmlp_bs, ln_g, ln_b):
    nc, consts = _get_program()
    f = lambda a: np.ascontiguousarray(np.asarray(a, dtype=np.float32))
    x_prompt, x_sample, c, cache_k, cache_v, c_ctx = map(f, (x_prompt, x_sample, c, cache_k, cache_v, c_ctx))
    lam4 = np.concatenate([f(lam_q1), f(lam_k1), f(lam_q2), f(lam_k2)], axis=1)
    shared = {
        "w_ada": f(w_ada), "b_ada": f(b_ada), "w_in": f(w_in), "w_out": f(w_out), "lam4": lam4,
        "subln_w": f(subln_w), "fourier_w": f(fourier_w), "gmlp_ws": f(gmlp_ws), "gmlp_bs": f(gmlp_bs),
        "ln_g": f(ln_g), "ln_b": f(ln_b),
    }
    shared.update(consts)
    in_maps = []
    for i in range(N_CORES):
        m = dict(shared)
        m["xs"] = x_sample[i]
        m["xp"] = x_prompt[2 * i:2 * i + 2]
        m["c2"] = np.stack([c[i], c_ctx], axis=0)
        m["ck"] = cache_k[i]
        m["cv"] = cache_v[i]
        in_maps.append(m)
    res = run_bass_kernel_spmd(nc, in_maps, core_ids=list(range(N_CORES)))
    r = res.results
    y_p = np.concatenate([r[i]["y_p"] for i in range(N_CORES)], axis=0).astype(np.float32)
    y_s = np.stack([r[i]["y_s"] for i in range(N_CORES)], axis=0).astype(np.float32)
    nk = np.concatenate([r[i]["nk"] for i in range(N_CORES)], axis=0).astype(np.float32)
    nv = np.concatenate([r[i]["nv"] for i in range(N_CORES)], axis=0).astype(np.float32)
    return (y_p, y_s, nk, nv)
```
